# Optimizing a Trainium2 kernel written in Bass

```python
import math
import jax, jax.numpy as jnp
from jax import lax
import numpy as np

D_MODEL = 1024
BATCH = 4
SEQ = 8192
DEPTH = 1
DEC_BATCH = 4
DEC_SEQ = 4096
PAST_LEN = 128

EPS = 1e-6
ROPE_THETA = 10000.0
BLOCK = 128
WINDOW = 128
HA = 8
KVA = 2
GA = HA // KVA
DA = 64
HB = 8
Q_RANK = 384
KV_RANK = 256
DN = 64
DR = 32
DV = 64
SPLIT_SIZES = (HA * DA, KVA * DA, KVA * DA, Q_RANK, KV_RANK, DR)
D_IN = sum(SPLIT_SIZES)
D_MIX = HA * DA + HB * DV
D_FF = int(math.ceil(8 * D_MODEL / 3 / 256) * 256)
NEG = -1e30

kernel_name = "hymba_swa_sink_mla_encoder"


def rms_norm(x, g):
    xf = x.astype(jnp.float32)
    y = xf * lax.rsqrt(jnp.mean(xf * xf, axis=-1, keepdims=True) + EPS)
    return (y * g.astype(jnp.float32)).astype(x.dtype)


def rope_tables(seq, dim):
    inv = 1.0 / (ROPE_THETA ** (jnp.arange(0, dim, 2, dtype=jnp.float32) / dim))
    ang = jnp.arange(seq, dtype=jnp.float32)[:, None] * inv[None, :]
    return jnp.cos(ang), jnp.sin(ang)


def apply_rope(x, cos, sin):
    xf = x.astype(jnp.float32)
    half = x.shape[-1] // 2
    x1, x2 = xf[..., :half], xf[..., half:]
    c = cos[None, :, None, :]
    s = sin[None, :, None, :]
    return jnp.concatenate([x1 * c - x2 * s, x2 * c + x1 * s], axis=-1).astype(x.dtype)


def window_gqa_sink(q, k, v, sink):
    B, S = q.shape[0], q.shape[1]
    nb = S // BLOCK
    qb = q.reshape(B, nb, BLOCK, KVA, GA, DA)
    pad = ((0, 0), (BLOCK, BLOCK), (0, 0), (0, 0))

    def bands(t):
        tb = jnp.pad(t, pad).reshape(B, nb + 2, BLOCK, KVA, DA)
        return jnp.concatenate([tb[:, :-2], tb[:, 1:-1], tb[:, 2:]], axis=2)

    kb, vb = bands(k), bands(v)
    s = jnp.einsum('bnqkgd,bnskd->bnkgqs', qb, kb).astype(jnp.float32) * (DA ** -0.5)
    blk = jnp.arange(nb)[:, None, None] * BLOCK
    qpos = blk + jnp.arange(BLOCK)[None, :, None]
    kpos = blk - BLOCK + jnp.arange(3 * BLOCK)[None, None, :]
    valid = (jnp.abs(qpos - kpos) <= WINDOW) & (kpos >= 0) & (kpos < S)
    s = jnp.where(valid[None, :, None, None], s, NEG)
    sink_l = sink.astype(jnp.float32).reshape(KVA, GA)[None, None, :, :, None, None]
    m = jnp.maximum(jnp.max(s, axis=-1, keepdims=True), sink_l)
    p = jnp.exp(s - m)
    denom = jnp.sum(p, axis=-1, keepdims=True) + jnp.exp(sink_l - m)
    o = jnp.einsum('bnkgqs,bnskd->bnqkgd', (p / denom).astype(v.dtype), vb)
    return o.reshape(B, S, HA * DA)


def mla(c_q, c_kv, k_rope, cq_g, w_uq, ckv_g, w_ukv, cos_r, sin_r):
    B, S = c_q.shape[0], c_q.shape[1]
    q = (rms_norm(c_q, cq_g) @ w_uq).reshape(B, S, HB, DN + DR)
    q_nope = q[..., :DN]
    q_rope = apply_rope(q[..., DN:], cos_r, sin_r)
    kv = (rms_norm(c_kv, ckv_g) @ w_ukv).reshape(B, S, HB, DN + DV)
    k_nope, v = kv[..., :DN], kv[..., DN:]
    k_r = apply_rope(k_rope[:, :, None, :], cos_r, sin_r)[:, :, 0]
    nb = S // BLOCK
    qn = q_nope.reshape(B, nb, BLOCK, HB, DN).transpose(1, 0, 2, 3, 4)
    qr = q_rope.reshape(B, nb, BLOCK, HB, DR).transpose(1, 0, 2, 3, 4)
    scale = (DN + DR) ** -0.5

    def block(args):
        qn_b, qr_b = args
        s = (jnp.einsum('bqhd,bshd->bhqs', qn_b, k_nope)
             + jnp.einsum('bqhr,bsr->bhqs', qr_b, k_r)).astype(jnp.float32) * scale
        p = jax.nn.softmax(s, axis=-1)
        return jnp.einsum('bhqs,bshd->bqhd', p.astype(v.dtype), v)

    o = lax.map(block, (qn, qr))
    return o.transpose(1, 0, 2, 3, 4).reshape(B, S, HB * DV)


def encoder_layer(x, cos_a, sin_a, cos_r, sin_r, g_mix, w_in, sink, cq_g, w_uq, ckv_g, w_ukv,
                  w_o, g_ffn, w_gate, w_up, w_down):
    B, S = x.shape[0], x.shape[1]
    h = rms_norm(x, g_mix)
    z = h @ w_in
    idx = list(np.cumsum(SPLIT_SIZES)[:-1])
    qa, ka, va, c_q, c_kv, k_rope = jnp.split(z, idx, axis=-1)
    qa = apply_rope(qa.reshape(B, S, HA, DA), cos_a, sin_a)
    ka = apply_rope(ka.reshape(B, S, KVA, DA), cos_a, sin_a)
    va = va.reshape(B, S, KVA, DA)
    o_a = window_gqa_sink(qa, ka, va, sink)
    o_b = mla(c_q, c_kv, k_rope, cq_g, w_uq, ckv_g, w_ukv, cos_r, sin_r)
    x = x + jnp.concatenate([o_a, o_b], axis=-1) @ w_o
    h = rms_norm(x, g_ffn)
    x = x + (jax.nn.silu(h @ w_gate) * (h @ w_up)) @ w_down
    return x


def trunk(x, g_mix, w_in, sink, cq_g, w_uq, ckv_g, w_ukv, w_o, g_ffn, w_gate, w_up, w_down, g_final):
    S = x.shape[1]
    cos_a, sin_a = rope_tables(S, DA)
    cos_r, sin_r = rope_tables(S, DR)
    for l in range(DEPTH):
        x = encoder_layer(x, cos_a, sin_a, cos_r, sin_r, g_mix[l], w_in[l], sink[l], cq_g[l], w_uq[l],
                          ckv_g[l], w_ukv[l], w_o[l], g_ffn[l], w_gate[l], w_up[l], w_down[l])
    return rms_norm(x, g_final)


def setup_inputs(seed: int = 0) -> dict:
    key = jax.random.key(seed)
    ks = jax.random.split(key, 16)
    f32 = jnp.float32

    def nrm(k, shape, scale):
        return jax.random.normal(k, shape, f32) * scale

    def gain(k, n):
        return 1.0 + 0.01 * jax.random.normal(k, (DEPTH, n), f32)

    return {
        "x_prompt": jax.random.normal(ks[0], (BATCH, SEQ, D_MODEL), f32),
        "x_sample": jax.random.normal(ks[1], (DEC_BATCH, DEC_SEQ, D_MODEL), f32),
        "g_mix": gain(ks[2], D_MODEL),
        "w_in": nrm(ks[3], (DEPTH, D_MODEL, D_IN), D_MODEL ** -0.5),
        "sink": nrm(ks[4], (DEPTH, HA), 0.5),
        "cq_g": gain(ks[5], Q_RANK),
        "w_uq": nrm(ks[6], (DEPTH, Q_RANK, HB * (DN + DR)), Q_RANK ** -0.5),
        "ckv_g": gain(ks[7], KV_RANK),
        "w_ukv": nrm(ks[8], (DEPTH, KV_RANK, HB * (DN + DV)), KV_RANK ** -0.5),
        "w_o": nrm(ks[9], (DEPTH, D_MIX, D_MODEL), D_MIX ** -0.5),
        "g_ffn": gain(ks[10], D_MODEL),
        "w_gate": nrm(ks[11], (DEPTH, D_MODEL, D_FF), D_MODEL ** -0.5),
        "w_up": nrm(ks[12], (DEPTH, D_MODEL, D_FF), D_MODEL ** -0.5),
        "w_down": nrm(ks[13], (DEPTH, D_FF, D_MODEL), D_FF ** -0.5),
        "g_final": 1.0 + 0.01 * jax.random.normal(ks[14], (D_MODEL,), f32),
    }


def reference(x_prompt, x_sample, g_mix, w_in, sink, cq_g, w_uq, ckv_g, w_ukv, w_o, g_ffn,
              w_gate, w_up, w_down, g_final):
    y_prompt = trunk(x_prompt, g_mix, w_in, sink, cq_g, w_uq, ckv_g, w_ukv, w_o, g_ffn,
                     w_gate, w_up, w_down, g_final)
    y_sample = trunk(x_sample, g_mix, w_in, sink, cq_g, w_uq, ckv_g, w_ukv, w_o, g_ffn,
                     w_gate, w_up, w_down, g_final)
    return (y_prompt, y_sample)
```

```python
import os
import numpy as np
from contextlib import ExitStack
import concourse.bass as bass
import concourse.mybir as mybir
from concourse.bass_utils import run_bass_kernel_spmd

F32 = mybir.dt.float32
BF16 = mybir.dt.bfloat16
AF = mybir.ActivationFunctionType
ALU = mybir.AluOpType

D = 1024
DIN = 1440
DFF = 2816
NJ = DFF // 128
EPS = 1e-6
HA, KVA, DA = 8, 2, 64
HB, QR, KVR, DN, DR, DV = 8, 384, 256, 64, 32, 64
SC_A = DA ** -0.5
SC_B = (DN + DR) ** -0.5
C_QA, C_CQ, C_KA, C_VA, C_CKV, C_KR = 0, 512, 896, 1024, 1152, 1408


class Res:
    __slots__ = ("name", "w", "r")

    def __init__(self, name):
        self.name = name
        self.w = {}
        self.r = {}


class Op:
    __slots__ = ("eng", "fn", "deps", "need_inc", "inc_val", "dma_sem", "dma_val", "key", "seq")


COMPUTE = ("pe", "act", "dve", "pool")


class Prog:
    def __init__(self, nc, es):
        self.nc = nc
        self.es = es
        self.ops = {e: [] for e in COMPUTE + ("sp",)}
        self.dma_sems = {}
        self.pending = {e: [] for e in COMPUTE + ("sp",)}
        self.sems = {e: es.enter_context(nc.semaphore("s_" + e)) for e in COMPUTE}
        self.uq = 0

    def uniq(self):
        self.uq += 1
        return f"u{self.uq}"

    def _key(self, o):
        return o.eng if o.dma_sem is None else ("dma", id(o.dma_sem))

    def op(self, eng, fn, reads=(), writes=(), dma_sem=None):
        o = Op()
        self.uq += 1
        o.seq = self.uq
        o.eng, o.fn, o.need_inc, o.inc_val = eng, fn, False, None
        o.dma_sem, o.dma_val = None, None
        if dma_sem is not None:
            ds = self.dma_sems.get(dma_sem)
            if ds is None:
                ds = [self.es.enter_context(self.nc.semaphore("dq_" + dma_sem)), 0]
                self.dma_sems[dma_sem] = ds
            ds[1] += 16
            o.dma_sem, o.dma_val = ds, ds[1]
        o.key = self._key(o)
        excl = [r for r in reads if r.name.startswith("bank")]
        if excl:
            writes = list(writes) + [r for r in excl if r not in writes]
            reads = [r for r in reads if not r.name.startswith("bank")]
        deps = {}

        def add(p, kind):
            if p is o:
                return
            if p.eng == eng and eng == "pe" and p.dma_sem is None:
                return
            deps[id(p)] = p

        for r in reads:
            for p in r.w.values():
                add(p, "raw")
        for w in writes:
            for p in w.w.values():
                add(p, "waw")
            for p in w.r.values():
                add(p, "war")
        for p in self.pending[eng]:
            deps[id(p)] = p
        self.pending[eng] = []
        for r in reads:
            r.r[o.key] = o
        for w in writes:
            w.w[o.key] = o
            w.r = {}
        o.deps = list(deps.values())
        for p in o.deps:
            p.need_inc = True
        self.ops[eng].append(o)
        return o

    def barrier(self):
        last = [self.ops[e][-1] for e in self.ops if self.ops[e]]
        seen = {}
        for e_ in self.ops:
            for o in self.ops[e_]:
                if o.dma_sem is not None:
                    seen[id(o.dma_sem)] = o
        last = [o for o in last if o.dma_sem is None] + list(seen.values())
        for e in self.pending:
            self.pending[e] = [p for p in last]

    def emit(self, block):
        nc = self.nc
        sems = self.sems
        for e in COMPUTE:
            n = 0
            for o in self.ops[e]:
                if o.need_inc and o.dma_sem is None:
                    n += 1
                    o.inc_val = n

        def run(eng_name, eng):
            waited = {}
            for o in self.ops[eng_name]:
                for p in o.deps:
                    if p.dma_sem is not None:
                        s, v = p.dma_sem[0], p.dma_val
                    else:
                        s, v = sems[p.eng], p.inc_val
                    k = id(s)
                    if waited.get(k, 0) >= v:
                        continue
                    waited[k] = v
                    eng.wait_ge(s, v)
                ins = o.fn(eng)
                if o.dma_sem is not None:
                    ins.then_inc(o.dma_sem[0], 16)
                elif o.need_inc:
                    ins.then_inc(sems[eng_name], 1)
            if eng_name == "sp":
                for ds in self.dma_sems.values():
                    if waited.get(id(ds[0]), 0) < ds[1]:
                        eng.wait_ge(ds[0], ds[1])

        @block.sync
        def _(e):
            run("sp", e)

        @block.tensor
        def _(e):
            run("pe", e)

        @block.scalar
        def _(e):
            run("act", e)

        @block.vector
        def _(e):
            run("dve", e)

        @block.gpsimd
        def _(e):
            run("pool", e)


class Arena:
    def __init__(self, ap, nbytes):
        self.ap = ap
        self.nbytes = nbytes
        self.top = 0
        self.peak = 0
        self.hist = []

    def _inherit(self, res, s0, e0):
        for (s1, e1, r1) in self.hist:
            if s1 < e0 and s0 < e1:
                for k, o in r1.w.items():
                    if k not in res.w or res.w[k].seq < o.seq:
                        res.w[k] = o
                for k, o in r1.r.items():
                    if k not in res.r or res.r[k].seq < o.seq:
                        res.r[k] = o

    def register(self, owner_res, extra_res):
        for (s1, e1, r1) in list(self.hist):
            if r1 is owner_res:
                self._inherit(extra_res, s1, e1)
                self.hist.append((s1, e1, extra_res))

    def alloc(self, name, free_shape, dtype, slots=1):
        esz = 4 if dtype == F32 else 2
        n = int(np.prod(free_shape))
        out = []
        for s in range(slots):
            off = (self.top + 63) // 64 * 64
            self.top = off + n * esz
            assert self.top <= self.nbytes, f"SBUF arena overflow at {name}: {self.top} > {self.nbytes}"
            v = self.ap[:, off // 2: off // 2 + n * esz // 2]
            if dtype == F32:
                v = v.bitcast(F32)
            if len(free_shape) == 2:
                v = v.rearrange("p (a b) -> p a b", b=free_shape[1])
            elif len(free_shape) == 3:
                v = v.rearrange("p (a b c) -> p a b c", b=free_shape[1], c=free_shape[2])
            res_ = Res(f"{name}{s}")
            self._inherit(res_, off, self.top)
            self.hist.append((off, self.top, res_))
            out.append((v, res_))
        self.peak = max(self.peak, self.top)
        return out

    def mark(self):
        return self.top

    def release(self, m):
        self.top = m


def build_program(SP, SS, TFFN=512):
    seqs = [dict(tag="p", S=SP, NQ=SP // 2), dict(tag="s", S=SS, NQ=SS // 2)]
    nc = bass.Bass("TRN2", target_bir_lowering=False)

    def din(name, shape, dt=F32):
        return nc.dram_tensor(name, list(shape), dt, kind="ExternalInput").ap()

    def dscr(name, shape, dt=BF16):
        return nc.dram_tensor(name, list(shape), dt, kind="Internal").ap()

    for sq in seqs:
        t, S, NQ = sq["tag"], sq["S"], sq["NQ"]
        sq["x"] = din("x" + t, [S, D])
        sq["ropeA"] = din("ropeA_" + t, [NQ + 128, 64])
        sq["ropeR"] = din("ropeR_" + t, [S, 32])
        sq["qrc"] = din("qrc_" + t, [32, NQ])
        sq["qrs"] = din("qrs_" + t, [32, NQ])
        sq["y"] = nc.dram_tensor("y" + t, [NQ, D], F32, kind="ExternalOutput").ap()
    w_in = din("w_in", [D, DIN])
    w_uqc = din("w_uqc", [QR, 1024])
    w_ukv = din("w_ukv", [KVR, 1024])
    w_o = din("w_o", [D, D])
    w_gate = din("w_gate", [D, DFF])
    w_up = din("w_up", [D, DFF])
    w_down = din("w_down", [DFF, D])
    gcols = din("gcols", [128, 21])
    g_final = din("g_final", [D])
    sink = din("sink", [8])
    masks_d = din("masks", [128, 4, 128])
    s_win = dscr("s_win", [128, 8 * DIN])
    s_wuq = dscr("s_wuq", [128, 3 * 1024])
    s_wukv = dscr("s_wukv", [128, 2 * 1024])
    s_wo = dscr("s_wo", [128, 8 * D])
    s_wg = dscr("s_wg", [NJ, 128, D])
    s_wu = dscr("s_wu", [NJ, 128, D])
    s_wd = dscr("s_wd", [NJ, 128, D])

    es = ExitStack()
    with es:
        ARENA_BYTES = 212736
        arena_t = es.enter_context(nc.sbuf_tensor("arena", [128, ARENA_BYTES // 2], BF16))
        ps_t = es.enter_context(nc.psum_tensor("ps", [128, 4096], F32))
        P = Prog(nc, es)
        A = Arena(arena_t[:, :], ARENA_BYTES)
        bankres = [Res(f"bank{b}") for b in range(8)]

        def bank(b):
            return ps_t[:, b * 512:(b + 1) * 512]

        def bank_bf(b):
            return ps_t[:, b * 512:(b + 1) * 512].bitcast(BF16)

        (ident, r_ident), = A.alloc("ident", [128], BF16)
        (esink, r_esink), = A.alloc("esink", [8], F32)
        (esk, r_esk), = A.alloc("esk", [8, 128], BF16)
        (sel1, r_sel1), = A.alloc("sel1", [128], BF16)
        (gc, r_gc), = A.alloc("gc", [21], F32)
        (stat, _), = A.alloc("stat", [64], F32)
        (epst, r_epst), = A.alloc("epst", [1], F32)
        stat_res = [Res(f"stat{i}") for i in range(64)]

        def st(i):
            return stat[:, i:i + 1], stat_res[i]

        P.op("pool", lambda e: e.memset(ident, 0.0), writes=[r_ident])
        P.op("pool", lambda e: e.affine_select(out=ident, in_=ident, pattern=[[-1, 128]],
                                               compare_op=ALU.not_equal, fill=1.0, base=0,
                                               channel_multiplier=1), reads=[r_ident], writes=[r_ident])
        P.op("pool", lambda e: e.memset(epst, EPS), writes=[r_epst])
        P.op("pool", lambda e: e.memset(sel1[0:1, 0:64], 0.0), writes=[r_sel1])
        P.op("pool", lambda e: e.memset(sel1[0:1, 64:128], 1.0), writes=[r_sel1])
        P.op("sp", lambda e: e.dma_start(out=gc, in_=gcols[:, :]), writes=[r_gc], dma_sem=P.uniq())
        P.op("sp", lambda e: e.dma_start(out=esink[0:1, :], in_=sink.partition_broadcast(1)),
             writes=[r_esink], dma_sem=P.uniq())
        P.op("act", lambda e: e.activation(out=esink[0:1, :], in_=esink[0:1, :], func=AF.Exp),
             reads=[r_esink], writes=[r_esink])
        P.op("dve", lambda e: e.tensor_copy(out=esk[0:1, :, :],
                                            in_=esink[0:1, :].unsqueeze(2).to_broadcast([1, 8, 128])),
             reads=[r_esink], writes=[r_esk])

        m0 = A.mark()
        stg_f = A.alloc("stg_f", [DFF], F32, slots=2)
        stg_b = A.alloc("stg_b", [DFF], BF16, slots=2)
        cnt = [0]

        scr = {}

        def convert(src_ap, ncols, gcol, dst_ap, perm=None, tag="x"):
            k = cnt[0]
            rs_ = Res(f"scr_{tag}{k}")
            scr.setdefault(tag, []).append(rs_)
            cnt[0] += 1
            (sf, rf), (sb, rb) = stg_f[k % 2], stg_b[k % 2]
            P.op("sp", lambda e: e.dma_start(out=sf[:, 0:ncols], in_=src_ap), writes=[rf], dma_sem=f"p0l{k % 2}")
            eng = "dve" if (k % 2 == 0 or tag != "win") else "act"
            pieces = perm if perm is not None else [(sb[:, 0:ncols], sf[:, 0:ncols])]
            for (o_ap, i_ap) in pieces:
                if gcol is None:
                    if eng == "dve":
                        P.op("dve", lambda e, o_ap=o_ap, i_ap=i_ap: e.tensor_copy(out=o_ap, in_=i_ap),
                             reads=[rf], writes=[rb])
                    else:
                        P.op("act", lambda e, o_ap=o_ap, i_ap=i_ap: e.copy(out=o_ap, in_=i_ap),
                             reads=[rf], writes=[rb])
                else:
                    if eng == "dve":
                        P.op("dve", lambda e, o_ap=o_ap, i_ap=i_ap: e.tensor_scalar(
                            out=o_ap, in0=i_ap, scalar1=gcol, scalar2=None, op0=ALU.mult),
                            reads=[rf, r_gc], writes=[rb])
                    else:
                        P.op("act", lambda e, o_ap=o_ap, i_ap=i_ap: e.activation(
                            out=o_ap, in_=i_ap, func=AF.Copy, scale=gcol),
                            reads=[rf, r_gc], writes=[rb])
            P.op("sp", lambda e: e.dma_start(out=dst_ap, in_=sb[:, 0:ncols] if dst_ap.ndim == 2 else
                                             sb[:, 0:ncols].rearrange("p (j f) -> p j f", f=128)),
                 reads=[rb], writes=[rs_], dma_sem=f"p0s{k % 2}")

        for c in range(8):
            k = cnt[0]
            sf, sb = stg_f[k % 2][0], stg_b[k % 2][0]
            perm = [
                (sb[:, 0:512].rearrange("p (h g d) -> p g h d", h=4, g=2),
                 sf[:, 0:512].rearrange("p (g h d) -> p g h d", g=2, h=4)),
                (sb[:, C_CQ:C_CQ + 384], sf[:, 768:1152]),
                (sb[:, C_KA:C_KA + 256], sf[:, 512:768]),
                (sb[:, C_CKV:C_CKV + 288], sf[:, 1152:1440]),
            ]
            convert(w_in[c * 128:(c + 1) * 128, :], DIN, gc[:, c:c + 1], s_win[:, c * DIN:(c + 1) * DIN], perm, tag="win")
        def conv_rest():
            for c in range(3):
                convert(w_uqc[c * 128:(c + 1) * 128, :], 1024, gc[:, 16 + c:17 + c],
                        s_wuq[:, c * 1024:(c + 1) * 1024], tag="wuq")
                yield
            for c in range(2):
                convert(w_ukv[c * 128:(c + 1) * 128, :], 1024, gc[:, 19 + c:20 + c],
                        s_wukv[:, c * 1024:(c + 1) * 1024], tag="wukv")
                yield
            for c in range(8):
                convert(w_o[c * 128:(c + 1) * 128, :], D, None, s_wo[:, c * D:(c + 1) * D], tag="wo")
                yield
            for (wsrc, sdst, tg) in ((w_gate, s_wg, "wg"), (w_up, s_wu, "wu")):
                dv = sdst.rearrange("j p (c f) -> c p j f", c=8)
                for c in range(8):
                    convert(wsrc[c * 128:(c + 1) * 128, :], DFF, gc[:, 8 + c:9 + c], dv[c], tag=tg)
                    yield
            for j in range(NJ):
                convert(w_down[j * 128:(j + 1) * 128, :], D, None, s_wd[j], tag="wd")
                yield

        bg = conv_rest()

        def do_sequence(sq):
            S, NQ = sq["S"], sq["NQ"]
            nb, nt = NQ // 128, S // 128
            QB = min(512, NQ)
            nqb = NQ // QB
            x_d, y_d = sq["x"], sq["y"]
            mseq = A.mark()
            (ocat, r_ocat), = A.alloc("ocat", [8, NQ], BF16)
            m_ocat = A.mark()
            (ckvT, r_ckvT), = A.alloc("ckvT", [2, S], BF16)
            (KT, r_KT), = A.alloc("KT", [S], BF16)
            r_KTr = Res("KTr")
            A.register(r_KT, r_KTr)
            (cqT, r_cqT), = A.alloc("cqT", [3, NQ], BF16)

            NXT = 2
            m1 = A.mark()
            (win, r_win), = A.alloc("win", [8, DIN], BF16)
            (msk, r_msk), = A.alloc("msk", [4, 512], BF16)
            xt = A.alloc("xt", [D], F32, slots=NXT)
            (junk, r_junk), = A.alloc("junk", [D], BF16)
            xn = A.alloc("xn", [D], BF16, slots=2)
            xnT = A.alloc("xnT", [8, 128], BF16, slots=2)
            ckvn = A.alloc("ckvn", [256], BF16, slots=2)
            krst = A.alloc("krst", [128], BF16, slots=2)
            rA = A.alloc("rA", [64], F32, slots=2)
            rR = A.alloc("rR", [32], F32, slots=2)
            (tAq, r_tAq), = A.alloc("tAq", [512], F32)
            mskf, r_mskf = tAq.rearrange("p (a b) -> p a b", a=4), r_tAq
            (tBq, r_tBq), = A.alloc("tBq", [512], F32)
            (tAk, r_tAk), = A.alloc("tAk", [128], F32)
            (tBk, r_tBk), = A.alloc("tBk", [128], F32)
            (tAr, r_tAr), = A.alloc("tAr", [32], F32)
            (tBr, r_tBr), = A.alloc("tBr", [32], F32)
            kab = A.alloc("kab", [128], BF16, slots=2)
            qab = A.alloc("qab", [512], BF16, slots=2)
            cqn = A.alloc("cqn", [384], BF16, slots=2)
            KAT = A.alloc("KAT", [128], BF16, slots=5)
            VA = A.alloc("VA", [2, 128], BF16, slots=5)
            QAT = A.alloc("QAT", [4, 128], BF16, slots=3)
            PTw = A.alloc("PTw", [512], BF16, slots=3)
            (densb, r_densb), = A.alloc("densb", [512], F32)
            rden, r_rden = densb, r_densb

            P.op("sp", lambda e: e.dma_start(out=win, in_=s_win.rearrange("p (c n) -> p c n", c=8)),
                 reads=scr["win"], writes=[r_win], dma_sem=P.uniq())
            P.op("sp", lambda e: e.dma_start(out=mskf, in_=masks_d), writes=[r_mskf], dma_sem=P.uniq())
            for k_ in range(4):
                P.op("dve", lambda e, k_=k_: e.tensor_scalar(
                    out=msk[:, k_, :].rearrange("p (a b) -> p a b", a=4),
                    in0=mskf[:, k_, :].unsqueeze(1).to_broadcast([128, 4, 128]),
                    scalar1=-1.0, scalar2=30000.0, op0=ALU.add, op1=ALU.mult), reads=[r_mskf], writes=[r_msk])
            for s_ in range(2):
                P.op("pool", lambda e, s_=s_: e.memset(krst[s_][0], 0.0), writes=[krst[s_][1]])
            for s_ in range(5):
                P.op("pool", lambda e, s_=s_: e.memset(VA[s_][0], 1.0), writes=[VA[s_][1]])

            def rstd_chain(src_ap, src_res, ncols, base, extra_reads=()):
                (a0, r0), (a1, r1), (a3, r3) = st(base), st(base + 1), st(base + 3)
                P.op("act", lambda e: e.activation(out=junk[:, 0:ncols], in_=src_ap, func=AF.Square, accum_out=a0),
                     reads=[src_res], writes=[r_junk, r0])
                P.op("act", lambda e: e.activation(out=a1, in_=a0, func=AF.Ln, scale=1.0 / ncols, bias=epst),
                     reads=[r0, r_epst], writes=[r1])
                P.op("act", lambda e: e.activation(out=a3, in_=a1, func=AF.Exp, scale=-0.5), reads=[r1], writes=[r3])
                return a3, r3

            def rope_tok(src4, src_res, cos_b, sin_b, tab_res, tA, r_tA, tB, r_tB, dst4, dst_res, shp):
                tA4 = tA.rearrange("p (a b c) -> p a b c", b=2, c=shp[2]) if tA.ndim == 2 else tA
                tB4 = tB.rearrange("p (a b c) -> p a b c", b=2, c=shp[2]) if tB.ndim == 2 else tB
                P.op("dve", lambda e: e.tensor_tensor(out=tA4, in0=src4, in1=cos_b, op=ALU.mult),
                     reads=[src_res, tab_res], writes=[r_tA])
                P.op("dve", lambda e: e.tensor_tensor(out=tB4, in0=src4, in1=sin_b, op=ALU.mult),
                     reads=[src_res, tab_res], writes=[r_tB])
                P.op("dve", lambda e: e.tensor_tensor(out=dst4[:, :, 0, :], in0=tA4[:, :, 0, :], in1=tB4[:, :, 1, :],
                                                      op=ALU.subtract), reads=[r_tA, r_tB], writes=[dst_res])
                P.op("dve", lambda e: e.tensor_tensor(out=dst4[:, :, 1, :], in0=tA4[:, :, 1, :], in1=tB4[:, :, 0, :],
                                                      op=ALU.add), reads=[r_tA, r_tB], writes=[dst_res])

            def slot_of(tile):
                return 4 if tile == 0 else (tile % 4)

            def stageA(i):
                (x_, rx), (xn_, rxn), (xT_, rxT) = xt[i % NXT], xn[i % 2], xnT[i % 2]
                P.op("sp", lambda e: e.dma_start(out=x_, in_=x_d[i * 128:(i + 1) * 128, :]), writes=[rx],
                     dma_sem=f"xt{i % NXT}")
                rs, rrs = rstd_chain(x_, rx, D, (i % 2) * 4)
                yield
                P.op("dve", lambda e: e.tensor_scalar(out=xn_, in0=x_, scalar1=rs, scalar2=None, op0=ALU.mult),
                     reads=[rx, rrs], writes=[rxn])
                yield

                def tr(e):
                    pb = bank_bf(0)
                    for c in range(8):
                        ins = e.transpose(out=pb[:, c * 128:(c + 1) * 128], in_=xn_[:, c * 128:(c + 1) * 128],
                                          identity=ident)
                    return ins
                P.op("pe", tr, reads=[rxn, r_ident], writes=[bankres[0]])
                yield
                P.op("act", lambda e: e.copy(out=xT_, in_=bank_bf(0).rearrange("p (c t) -> p c t", c=8)),
                     reads=[bankres[0]], writes=[rxT])
                yield

            def mm_group(e, out_ap, xT_, c0, ncols):
                for c in range(8):
                    ins = e.matmul(out_ap, lhsT=xT_[:, c, :], rhs=win[:, c, c0:c0 + ncols],
                                   start=(c == 0), stop=(c == 7))
                return ins

            def stageB(i):
                own = 1 <= i <= nb
                hown = i <= nb
                (xT_, rxT) = xnT[i % 2]
                tok0 = i * 128
                psT8 = bank_bf(4).rearrange("p (a t) -> p a t", a=8)
                psTa = psT8[:, 0:4, :]
                psTb = psT8
                if hown:
                    P.op("pe", lambda e: mm_group(e, bank(3)[:, 0:416], xT_, C_VA, 416), reads=[rxT, r_win],
                         writes=[bankres[3]])
                    P.op("pe", lambda e: mm_group(e, bank(2)[:, 0:512], xT_, C_CQ, 512), reads=[rxT, r_win],
                         writes=[bankres[2]])
                else:
                    P.op("pe", lambda e: mm_group(e, bank(3)[:, 128:416], xT_, C_CKV, 288), reads=[rxT, r_win],
                         writes=[bankres[3]])
                if own:
                    P.op("pe", lambda e: mm_group(e, bank(1)[:, 0:512], xT_, C_QA, 512), reads=[rxT, r_win],
                         writes=[bankres[1]])
                yield
                (ck_, rck), (kr_, rkr) = ckvn[i % 2], krst[i % 2]
                rs, rrs = rstd_chain(bank(3)[:, 128:384], bankres[3], 256, 8 + (i % 2) * 4)
                P.op("dve", lambda e: e.tensor_scalar(out=ck_, in0=bank(3)[:, 128:384], scalar1=rs, scalar2=None,
                                                      op0=ALU.mult), reads=[bankres[3], rrs], writes=[rck])
                yield
                (rr_, rrr) = rR[i % 2]
                P.op("sp", lambda e: e.dma_start(out=rr_, in_=sq["ropeR"][tok0:tok0 + 128, :]), writes=[rrr],
                     dma_sem=f"rR{i % 2}")
                src4 = bank(3)[:, 384:416].rearrange("p (a b c) -> p a b c", a=1, b=2)
                cos_b = rr_[:, 0:16].unsqueeze(1).unsqueeze(1).to_broadcast([128, 1, 2, 16])
                sin_b = rr_[:, 16:32].unsqueeze(1).unsqueeze(1).to_broadcast([128, 1, 2, 16])
                dst4 = kr_[:, 64:96].rearrange("p (a b c) -> p a b c", a=1, b=2)
                rope_tok(src4, bankres[3], cos_b, sin_b, rrr, tAr, r_tAr, tBr, r_tBr, dst4, rkr, (1, 2, 16))

                yield
                def trB(e):
                    e.transpose(out=psTa[:, 0, :], in_=ck_[:, 0:128], identity=ident)
                    e.transpose(out=psTa[:, 1, :], in_=ck_[:, 128:256], identity=ident)
                    return e.transpose(out=psTa[:, 2, :], in_=kr_[:, 0:128], identity=ident)
                P.op("pe", trB, reads=[rck, rkr, r_ident], writes=[bankres[4]])
                yield
                P.op("act", lambda e: e.copy(out=ckvT[:, :, tok0:tok0 + 128], in_=psTa[:, 0:2, :]),
                     reads=[bankres[4]], writes=[r_ckvT])
                P.op("act", lambda e: e.copy(out=KT[64:96, tok0:tok0 + 128], in_=psTa[64:96, 2, :]),
                     reads=[bankres[4]], writes=[r_KTr])
                yield
                if hown:
                    sl = slot_of(i)
                    (ra_, rra) = rA[i % 2]
                    P.op("sp", lambda e: e.dma_start(out=ra_, in_=sq["ropeA"][tok0:tok0 + 128, :]), writes=[rra],
                         dma_sem=f"rA{i % 2}")
                    (kab_, rkab) = kab[i % 2]
                    src4 = bank(2)[:, 384:512].rearrange("p (a b c) -> p a b c", a=2, b=2)
                    cos_b = ra_[:, 0:32].unsqueeze(1).unsqueeze(1).to_broadcast([128, 2, 2, 32])
                    sin_b = ra_[:, 32:64].unsqueeze(1).unsqueeze(1).to_broadcast([128, 2, 2, 32])
                    dst4 = kab_.rearrange("p (a b c) -> p a b c", a=2, b=2)
                    rope_tok(src4, bankres[2], cos_b, sin_b, rra, tAk, r_tAk, tBk, r_tBk, dst4, rkab, (2, 2, 32))
                    yield
                    (va_, rva), (kat_, rkat) = VA[sl], KAT[sl]
                    P.op("act", lambda e: e.copy(out=va_[:, :, 0:64],
                                                 in_=bank(3)[:, 0:128].rearrange("p (a b) -> p a b", a=2)),
                         reads=[bankres[3]], writes=[rva])
                    P.op("pe", lambda e: e.transpose(out=psTa[:, 3, :], in_=kab_, identity=ident),
                         reads=[rkab, r_ident], writes=[bankres[4]])
                    P.op("act", lambda e: e.copy(out=kat_, in_=psTa[:, 3, :]), reads=[bankres[4]], writes=[rkat])
                yield
                if own:
                    (qab_, rqab) = qab[i % 2]
                    (ra_, rra) = rA[i % 2]
                    src4 = bank(1)[:, 0:512].rearrange("p (a b c) -> p a b c", a=8, b=2)
                    cos_b = ra_[:, 0:32].unsqueeze(1).unsqueeze(1).to_broadcast([128, 8, 2, 32])
                    sin_b = ra_[:, 32:64].unsqueeze(1).unsqueeze(1).to_broadcast([128, 8, 2, 32])
                    dst4 = qab_.rearrange("p (a b c) -> p a b c", a=8, b=2)
                    rope_tok(src4, bankres[1], cos_b, sin_b, rra, tAq, r_tAq, tBq, r_tBq, dst4, rqab, (8, 2, 32))

                    def trQ(e):
                        for h in range(4):
                            ins = e.transpose(out=psTb[:, 4 + h, :], in_=qab_[:, h * 128:(h + 1) * 128], identity=ident)
                        return ins
                    yield
                    P.op("pe", trQ, reads=[rqab, r_ident], writes=[bankres[4]])
                    (qat_, rqat) = QAT[i % 3]
                    P.op("act", lambda e: e.copy(out=qat_, in_=psTb[:, 4:8, :]), reads=[bankres[4]], writes=[rqat])
                    yield
                    (cq_, rcq) = cqn[i % 2]
                    rs2, rrs2 = rstd_chain(bank(2)[:, 0:384], bankres[2], 384, 16 + (i % 2) * 4)
                    P.op("dve", lambda e: e.tensor_scalar(out=cq_, in0=bank(2)[:, 0:384], scalar1=rs2, scalar2=None,
                                                          op0=ALU.mult), reads=[bankres[2], rrs2], writes=[rcq])

                    def trC(e):
                        for c in range(3):
                            ins = e.transpose(out=psTb[:, 4 + c, :], in_=cq_[:, c * 128:(c + 1) * 128],
                                              identity=ident)
                        return ins
                    yield
                    P.op("pe", trC, reads=[rcq, r_ident], writes=[bankres[4]])
                    q0 = (i - 1) * 128
                    P.op("act", lambda e: e.copy(out=cqT[:, :, q0:q0 + 128], in_=psTb[:, 4:7, :]),
                         reads=[bankres[4]], writes=[r_cqT])

            def stageC(b):
                cur = b + 1
                prev = b
                nxt = b + 2 if b < nb - 1 else 0
                pm = 2 if b == 0 else 0
                nm = 3 if b == nb - 1 else 1
                (qat_, rqat) = QAT[cur % 3]
                tok0 = b * 128

                def unit_qk(kv, n_, tile, mk):
                    rows = slice(kv * 64, (kv + 1) * 64)
                    sl = slot_of(tile)
                    (kat_, rkat) = KAT[sl]
                    sb_ = 5 + (kv * 3 + n_) % 2

                    def qkm(e):
                        ins = e.matmul(bank(sb_), lhsT=kat_[rows, :], rhs=qat_[rows, :, :], start=True,
                                       stop=(mk is None))
                        if mk is not None:
                            ins = e.matmul(bank(sb_), lhsT=ident, rhs=msk[:, mk, :], start=False, stop=True)
                        return ins
                    P.op("pe", qkm, reads=[rkat, rqat, r_msk, r_ident], writes=[bankres[sb_]])

                def unit_pv(kv, n_, tile, mk):
                    sl = slot_of(tile)
                    (va_, rva) = VA[sl]
                    (pt_, rpt) = PTw[(b * 6 + kv * 3 + n_) % 3]
                    sb_ = 5 + (kv * 3 + n_) % 2
                    P.op("act", lambda e: e.activation(out=pt_, in_=bank(sb_), func=AF.Exp, scale=SC_A),
                         reads=[bankres[sb_]], writes=[rpt])

                    def pvm(e):
                        ins = e.matmul(bank(7), lhsT=va_[:, kv, :], rhs=pt_, start=(n_ == 0), stop=False)
                        if n_ == 2:
                            ins = e.matmul(bank(7), lhsT=sel1[0:1, :],
                                           rhs=esk[0:1, kv * 4:(kv + 1) * 4, :], start=False, stop=True)
                        return ins
                    P.op("pe", pvm, reads=[rva, rpt, r_sel1, r_esk], writes=[bankres[7]])

                def epi(kv):
                    P.op("act", lambda e: e.activation(out=densb[0:64, :], in_=bank(7)[64:128, :], func=AF.Ln),
                         reads=[bankres[7]], writes=[r_densb])
                    P.op("act", lambda e: e.activation(out=densb[0:64, :], in_=densb[0:64, :], func=AF.Exp, scale=-1.0),
                         reads=[r_densb], writes=[r_densb])
                    o4 = bank(7)[0:64, :].rearrange("p (a b) -> p a b", a=4)
                    b4 = densb[0:64, :].rearrange("p (a b) -> p a b", a=4)

                    def fin(half):
                        P.op("dve", lambda e: e.tensor_tensor(
                            out=ocat[half * 64:(half + 1) * 64, kv * 2:kv * 2 + 2, tok0:tok0 + 128],
                            in0=o4[:, half::2, :], in1=b4[:, half::2, :], op=ALU.mult),
                            reads=[bankres[7], r_densb], writes=[r_ocat])
                    fin(0)
                    fin(1)

                units = [(kv, n_, tile, mk) for kv in range(2)
                         for n_, (tile, mk) in enumerate([(prev, pm), (cur, None), (nxt, nm)])]
                unit_qk(*units[0])
                for i_, u_ in enumerate(units):
                    if i_ + 1 < len(units):
                        unit_qk(*units[i_ + 1])
                    unit_pv(*u_)
                    yield
                    if u_[1] == 2:
                        epi(u_[0])
                        yield

            def bgstep(n):
                for _ in range(n):
                    try:
                        next(bg)
                    except StopIteration:
                        return
                    yield

            def interleave(gens):
                gens = [g for g in gens if g is not None]
                while gens:
                    for g in list(gens):
                        try:
                            next(g)
                        except StopIteration:
                            gens.remove(g)

            for step in range(nt + 4):
                gl = []
                if step < nt:
                    gl.append(stageA(step))
                if 1 <= step <= nt:
                    gl.append(stageB(step - 1))
                b = step - 4
                if 0 <= b < nb:
                    gl.append(stageC(b))
                gl.append(bgstep(2))
                interleave(gl)
            for _ in bg:
                pass
            A.release(m1)

            m2 = A.mark()
            (wuq, r_wuq), = A.alloc("wuq", [3, 1024], BF16)
            r_QTq = [Res(f"QTq{q_}") for q_ in range(nqb)]
            (wukv, r_wukv), = A.alloc("wukv", [2, 1024], BF16)
            (Vh, r_Vh), = A.alloc("Vh", [nt, 128], BF16)
            (QT, r_QT), = A.alloc("QT", [NQ], BF16)
            for r__ in r_QTq:
                A.register(r_QT, r__)
            (qrc, r_qrc), = A.alloc("qrc", [NQ], F32)
            qrs, r_qrs = qrc, Res("qrs")
            A.register(r_qrc, r_qrs)
            (t1, r_t1), = A.alloc("t1", [QB], F32)
            (t2, r_t2), = A.alloc("t2", [QB], F32)
            PT = A.alloc("PT", [1024], BF16, slots=3)
            (osb, r_osb), = A.alloc("osb", [QB], F32)
            rden2, r_rden2 = osb, r_osb
            P.op("sp", lambda e: e.dma_start(out=wuq, in_=s_wuq.rearrange("p (c n) -> p c n", c=3)),
                 reads=scr["wuq"], writes=[r_wuq], dma_sem=P.uniq())
            P.op("sp", lambda e: e.dma_start(out=wukv, in_=s_wukv.rearrange("p (c n) -> p c n", c=2)),
                 reads=scr["wukv"], writes=[r_wukv], dma_sem=P.uniq())
            P.op("sp", lambda e: e.dma_start(out=qrc[64:96, :], in_=sq["qrc"][:, :]), writes=[r_qrc], dma_sem=P.uniq())
            P.op("sp", lambda e: e.dma_start(out=qrs[96:128, :], in_=sq["qrs"][:, :]), writes=[r_qrs], dma_sem=P.uniq())
            P.op("pool", lambda e: e.memset(Vh, 1.0), writes=[r_Vh])
            pbc = [0]
            ev = [0]

            def evac(out_ap, in_ap, reads, writes):
                P.op("dve", lambda e: e.tensor_copy(out=out_ap, in_=in_ap), reads=reads, writes=writes)

            pend = []

            def do_head(h):
                KBLK = min(512, S)

                def kgen():
                    for tb in range(S // KBLK):
                        pb = pbc[0] % 6
                        pbc[0] += 1

                        def kp(e, tb=tb, pb=pb):
                            for c in range(2):
                                ins = e.matmul(bank(pb)[0:64, 0:KBLK], lhsT=wukv[:, c, h * 128:h * 128 + 64],
                                               rhs=ckvT[:, c, tb * KBLK:(tb + 1) * KBLK], start=(c == 0), stop=(c == 1))
                            return ins
                        P.op("pe", kp, reads=[r_wukv, r_ckvT], writes=[bankres[pb]])
                        if tb % 2 == 0:
                            P.op("dve", lambda e, tb=tb, pb=pb: e.tensor_copy(
                                out=KT[0:64, tb * KBLK:(tb + 1) * KBLK], in_=bank(pb)[0:64, 0:KBLK]),
                                reads=[bankres[pb]], writes=[r_KT])
                        else:
                            P.op("act", lambda e, tb=tb, pb=pb: e.copy(
                                out=KT[0:64, tb * KBLK:(tb + 1) * KBLK], in_=bank(pb)[0:64, 0:KBLK]),
                                reads=[bankres[pb]], writes=[r_KT])
                        yield

                def vgen():
                    for g in range(nt // 8):
                        pb = pbc[0] % 6
                        pbc[0] += 1

                        def vp(e, g=g, pb=pb):
                            for t_ in range(8):
                                kt = g * 8 + t_
                                for c in range(2):
                                    ins = e.matmul(bank(pb)[:, t_ * 64:(t_ + 1) * 64],
                                                   lhsT=ckvT[:, c, kt * 128:(kt + 1) * 128],
                                                   rhs=wukv[:, c, h * 128 + 64:h * 128 + 128],
                                                   start=(c == 0), stop=(c == 1))
                            return ins
                        P.op("pe", vp, reads=[r_wukv, r_ckvT], writes=[bankres[pb]])
                        P.op("act", lambda e, g=g, pb=pb: e.copy(
                            out=Vh[:, g * 8:(g + 1) * 8, 0:64], in_=bank(pb).rearrange("p (a b) -> p a b", a=8)),
                            reads=[bankres[pb]], writes=[r_Vh])
                        yield
                        yield
                gens_ = [kgen(), vgen()]
                while gens_:
                    for g_ in list(gens_):
                        try:
                            next(g_)
                        except StopIteration:
                            gens_.remove(g_)
                def q_proj(qb, PB):
                    qs = slice(qb * QB, (qb + 1) * QB)

                    def qp(e):
                        for c in range(3):
                            ins = e.matmul(bank(PB)[:, 0:QB], lhsT=wuq[:, c, h * 128:(h + 1) * 128],
                                           rhs=cqT[:, c, qs], start=(c == 0), stop=(c == 2))
                        return ins
                    P.op("pe", qp, reads=[r_wuq, r_cqT], writes=[bankres[PB]])
                    if qb == 0:
                        P.op("act", lambda e: e.copy(out=QT[0:64, qs], in_=bank(PB)[0:64, 0:QB]),
                             reads=[bankres[PB]], writes=[r_QTq[qb]])
                    else:
                        P.op("dve", lambda e: e.tensor_copy(out=QT[0:64, qs], in_=bank(PB)[0:64, 0:QB]),
                             reads=[bankres[PB]], writes=[r_QTq[qb]])
                    P.op("dve", lambda e: e.tensor_tensor(out=t1[64:96, :], in0=bank(PB)[64:96, 0:QB],
                                                          in1=qrc[64:96, qs], op=ALU.mult),
                         reads=[bankres[PB], r_qrc], writes=[r_t1])
                    P.op("dve", lambda e: e.tensor_tensor(out=t2[64:96, :], in0=bank(PB)[96:128, 0:QB],
                                                          in1=qrs[96:128, qs], op=ALU.mult),
                         reads=[bankres[PB], r_qrs], writes=[r_t2])
                    P.op("dve", lambda e: e.tensor_tensor(out=QT[64:96, qs], in0=t1[64:96, :],
                                                          in1=t2[64:96, :], op=ALU.add),
                         reads=[r_t1, r_t2], writes=[r_QTq[qb]])
                q_proj(0, pbc[0] % 6)
                pbc[0] += 1
                npair = nt // 2
                for qb in range(nqb):
                    qs = slice(qb * QB, (qb + 1) * QB)
                    ob = 6 + (qb % 2)

                    def qk(p, qs=qs, qb=qb):
                        b0 = (p % 3) * 2

                        def f(e):
                            for u in range(2):
                                kt = 2 * p + u
                                ins = e.matmul(bank(b0 + u)[:, 0:QB], lhsT=KT[0:96, kt * 128:(kt + 1) * 128],
                                               rhs=QT[0:96, qs], start=True, stop=True)
                            return ins
                        P.op("pe", f, reads=[r_KT, r_KTr, r_QTq[qb]], writes=[bankres[b0], bankres[b0 + 1]])
                        (pt_, rpt) = PT[p % 3]
                        src = ps_t[:, b0 * 512:(b0 + 2) * 512].rearrange("p (a b) -> p a b", a=2)[:, :, 0:QB]
                        P.op("act", lambda e: e.activation(out=pt_.rearrange("p (a b) -> p a b", a=2)[:, :, 0:QB],
                                                           in_=src, func=AF.Exp, scale=SC_B),
                             reads=[bankres[b0], bankres[b0 + 1]], writes=[rpt])

                    def pv(p, ob=ob):
                        (pt_, rpt) = PT[p % 3]

                        def f(e):
                            for u in range(2):
                                kt = 2 * p + u
                                ins = e.matmul(bank(ob)[:, 0:QB], lhsT=Vh[:, kt, :],
                                               rhs=pt_[:, u * 512:u * 512 + QB],
                                               start=(kt == 0), stop=(kt == nt - 1))
                            return ins
                        P.op("pe", f, reads=[r_Vh, rpt], writes=[bankres[ob]])

                    def epilogue(ob=ob, qs=qs, h=h):
                        P.op("dve", lambda e: e.reciprocal(out=osb[0:64, :], in_=bank(ob)[64:128, 0:QB]),
                             reads=[bankres[ob]], writes=[r_osb])
                        half = h % 2
                        P.op("dve", lambda e: e.tensor_tensor(
                            out=ocat[half * 64:(half + 1) * 64, 4 + h // 2, qs], in0=bank(ob)[0:64, 0:QB],
                            in1=osb[0:64, :], op=ALU.mult),
                            reads=[r_osb, bankres[ob]], writes=[r_ocat])

                    for p in range(npair):
                        qk(p)
                        if p == 2 and pend:
                            pend.pop()()
                        if p == min(12, npair - 2) and qb + 1 < nqb:
                            q_proj(qb + 1, 6 + ((qb + 1) % 2))
                        if p >= 2:
                            pv(p - 2)
                    pv(npair - 2)
                    pv(npair - 1)
                    pend.append(epilogue)
                if pend:
                    pend.pop()()
            for h_ in range(HB):
                do_head(h_)
            A.release(m2)
            A.release(m_ocat)
            alloc3 = A.alloc

            T = min(TFFN, NQ)
            ntile = T // 128
            nhalf = max(1, T // 512)
            HW = min(512, T)
            (wo, r_wo), = alloc3("wo", [8, D], BF16)
            (gfin, r_gfin), = alloc3("gfin", [D], F32)
            h2Ts = alloc3("h2T", [8, T], BF16, slots=2)
            (actT, r_actT), = alloc3("actT", [NJ, T], BF16)
            xm = alloc3("xm", [D], F32, slots=2 * ntile)
            (junk3, r_junk3), = alloc3("junk3", [D], BF16)
            h2 = alloc3("h2", [D], BF16, slots=2)
            sg = alloc3("sg", [HW], F32, slots=2)
            wgs = alloc3("wgs", [8, 128], BF16, slots=4)
            wus = alloc3("wus", [8, 128], BF16, slots=4)
            wds = alloc3("wds", [512], BF16, slots=6)
            P.op("sp", lambda e: e.dma_start(out=wo, in_=s_wo.rearrange("p (c n) -> p c n", c=8)),
                 reads=scr["wo"], writes=[r_wo], dma_sem=P.uniq())
            P.op("sp", lambda e: e.dma_start(out=gfin, in_=g_final.partition_broadcast(128)), writes=[r_gfin],
                 dma_sem=P.uniq())
            wcnt = [0, 0]
            NSG = NQ // T

            def S1(sg_i):
                g0 = sg_i * T
                st_ = sg_i % 2
                (h2T_, r_h2T_) = h2Ts[st_]
                sb = 24 + st_ * 12
                for ti in range(ntile):
                    tok0 = g0 + ti * 128
                    (xm_, rxm) = xm[st_ * ntile + ti]
                    P.op("sp", lambda e, xm_=xm_, tok0=tok0: e.dma_start(
                        out=xm_, in_=x_d[128 + tok0:128 + tok0 + 128, :]), writes=[rxm],
                        dma_sem=f"xm{st_ * ntile + ti}")

                    def wo_mm(e, tok0=tok0):
                        for n in range(2):
                            for c in range(8):
                                ins = e.matmul(bank(n), lhsT=ocat[:, c, tok0:tok0 + 128],
                                               rhs=wo[:, c, n * 512:(n + 1) * 512], start=(c == 0), stop=(c == 7))
                        return ins
                    P.op("pe", wo_mm, reads=[r_ocat, r_wo], writes=[bankres[0], bankres[1]])
                    yield
                    P.op("dve", lambda e, xm_=xm_: e.tensor_tensor(out=xm_, in0=ps_t[:, 0:1024], in1=xm_, op=ALU.add),
                         reads=[bankres[0], bankres[1], rxm], writes=[rxm])
                    (a0, r0) = st(sb + ti)
                    P.op("act", lambda e, xm_=xm_, a0=a0: e.activation(out=junk3, in_=xm_, func=AF.Square, accum_out=a0),
                         reads=[rxm], writes=[r_junk3, r0])
                    yield
                rl = [stat_res[sb + k] for k in range(ntile)]
                rm = [stat_res[sb + 4 + k] for k in range(ntile)]
                P.op("dve", lambda e: e.tensor_scalar(out=stat[:, sb + 4:sb + 4 + ntile], in0=stat[:, sb:sb + ntile],
                                                      scalar1=1.0 / D, scalar2=EPS, op0=ALU.mult, op1=ALU.add),
                     reads=rl, writes=rm)
                P.op("act", lambda e: e.activation(out=stat[:, sb + 4:sb + 4 + ntile], in_=stat[:, sb + 4:sb + 4 + ntile],
                                                   func=AF.Sqrt), reads=rm, writes=rm)
                P.op("dve", lambda e: e.reciprocal(out=stat[:, sb + 8:sb + 8 + ntile], in_=stat[:, sb + 4:sb + 4 + ntile]),
                     reads=rm, writes=[stat_res[sb + 8 + k] for k in range(ntile)])
                yield
                for ti in range(ntile):
                    (xm_, rxm) = xm[st_ * ntile + ti]
                    (h2_, rh2) = h2[ti % 2]
                    rs, rrs = st(sb + 8 + ti)
                    P.op("act", lambda e, xm_=xm_, h2_=h2_, rs=rs: e.activation(
                        out=h2_, in_=xm_, func=AF.Copy, scale=rs), reads=[rxm, rrs], writes=[rh2])
                    yield

                    def tr3(e, h2_=h2_):
                        pb = bank_bf(2)
                        for c in range(8):
                            ins = e.transpose(out=pb[:, c * 128:(c + 1) * 128], in_=h2_[:, c * 128:(c + 1) * 128],
                                              identity=ident)
                        return ins
                    P.op("pe", tr3, reads=[rh2, r_ident], writes=[bankres[2]])
                    yield
                    P.op("act", lambda e, ti=ti: e.copy(out=h2T_[:, :, ti * 128:(ti + 1) * 128],
                                                       in_=bank_bf(2).rearrange("p (c t) -> p c t", c=8)),
                         reads=[bankres[2]], writes=[r_h2T_])
                    yield

            def S2(sg_i):
                st_ = sg_i % 2
                (h2T_, r_h2T_) = h2Ts[st_]
                for j in range(NJ):
                    k = wcnt[0]
                    wcnt[0] += 1
                    (wg_, rwg), (wu_, rwu) = wgs[k % 4], wus[k % 4]
                    P.op("sp", lambda e, wg_=wg_, j=j: e.dma_start(
                        out=wg_, in_=s_wg[j].rearrange("p (c f) -> p c f", c=8)), reads=scr["wg"], writes=[rwg],
                        dma_sem=f"wg{k % 4}")
                    P.op("sp", lambda e, wu_=wu_, j=j: e.dma_start(
                        out=wu_, in_=s_wu[j].rearrange("p (c f) -> p c f", c=8)), reads=scr["wu"], writes=[rwu],
                        dma_sem=f"wu{k % 4}")
                    for hf in range(nhalf):
                        u = (j * nhalf + hf) % 2
                        gb, ub = 4 + u * 2, 5 + u * 2
                        ts_ = slice(hf * HW, (hf + 1) * HW)

                        def gmm(e, wg_=wg_, gb=gb, ts_=ts_):
                            for c in range(8):
                                ins = e.matmul(bank(gb)[:, 0:HW], lhsT=wg_[:, c, :], rhs=h2T_[:, c, ts_],
                                               start=(c == 0), stop=(c == 7))
                            return ins

                        def umm(e, wu_=wu_, ub=ub, ts_=ts_):
                            for c in range(8):
                                ins = e.matmul(bank(ub)[:, 0:HW], lhsT=wu_[:, c, :], rhs=h2T_[:, c, ts_],
                                               start=(c == 0), stop=(c == 7))
                            return ins
                        P.op("pe", gmm, reads=[rwg, r_h2T_], writes=[bankres[gb]])
                        P.op("pe", umm, reads=[rwu, r_h2T_], writes=[bankres[ub]])
                        (sg_, rsg) = sg[u]
                        P.op("act", lambda e, sg_=sg_, gb=gb: e.activation(out=sg_, in_=bank(gb)[:, 0:HW], func=AF.Silu),
                             reads=[bankres[gb]], writes=[rsg])
                        P.op("dve", lambda e, sg_=sg_, ub=ub, j=j, ts_=ts_: e.tensor_tensor(
                            out=actT[:, j, ts_], in0=sg_, in1=bank(ub)[:, 0:HW], op=ALU.mult),
                            reads=[rsg, bankres[ub]], writes=[r_actT])
                        yield

            def S3(sg_i):
                g0 = sg_i * T
                st_ = sg_i % 2
                sb = 48 + st_ * 8
                for q4 in range((ntile + 3) // 4):
                    tiles = list(range(q4 * 4, min(ntile, q4 * 4 + 4)))
                    for n in range(2):
                        for j in range(NJ):
                            k = wcnt[1]
                            wcnt[1] += 1
                            (wd_, rwd) = wds[k % 6]
                            P.op("sp", lambda e, wd_=wd_, j=j, n=n: e.dma_start(
                                out=wd_, in_=s_wd[j, :, n * 512:(n + 1) * 512]), reads=scr["wd"], writes=[rwd],
                                dma_sem=f"wd{k % 6}")

                            def dmm(e, wd_=wd_, j=j, tiles=tiles):
                                for a_, ti in enumerate(tiles):
                                    ins = e.matmul(bank(a_), lhsT=actT[:, j, ti * 128:(ti + 1) * 128], rhs=wd_,
                                                   start=(j == 0), stop=(j == NJ - 1))
                                return ins
                            P.op("pe", dmm, reads=[rwd, r_actT], writes=[bankres[a2] for a2 in range(len(tiles))])
                        for a_, ti in enumerate(tiles):
                            (xm_, rxm) = xm[st_ * ntile + ti]
                            P.op("dve", lambda e, xm_=xm_, a_=a_, n=n: e.tensor_tensor(
                                out=xm_[:, n * 512:(n + 1) * 512], in0=bank(a_), in1=xm_[:, n * 512:(n + 1) * 512],
                                op=ALU.add), reads=[bankres[a_], rxm], writes=[rxm])
                    nt_ = len(tiles)
                    for a_, ti in enumerate(tiles):
                        (xm_, rxm) = xm[st_ * ntile + ti]
                        (a0, r0) = st(sb + a_)
                        P.op("act", lambda e, xm_=xm_, a0=a0: e.activation(out=junk3, in_=xm_, func=AF.Square,
                                                                          accum_out=a0),
                             reads=[rxm], writes=[r_junk3, r0])
                    rl = [stat_res[sb + k] for k in range(nt_)]
                    rm = [stat_res[sb + 4 + k] for k in range(nt_)]
                    P.op("dve", lambda e: e.tensor_scalar(out=stat[:, sb + 4:sb + 4 + nt_], in0=stat[:, sb:sb + nt_],
                                                          scalar1=1.0 / D, scalar2=EPS, op0=ALU.mult, op1=ALU.add),
                         reads=rl, writes=rm)
                    P.op("act", lambda e: e.activation(out=stat[:, sb + 4:sb + 4 + nt_], in_=stat[:, sb + 4:sb + 4 + nt_],
                                                       func=AF.Sqrt), reads=rm, writes=rm)
                    P.op("dve", lambda e: e.reciprocal(out=stat[:, sb:sb + nt_], in_=stat[:, sb + 4:sb + 4 + nt_]),
                         reads=rm, writes=rl)
                    for a_, ti in enumerate(tiles):
                        (xm_, rxm) = xm[st_ * ntile + ti]
                        tok0 = g0 + ti * 128
                        rs, rrs = st(sb + a_)
                        P.op("dve", lambda e, xm_=xm_, rs=rs: e.scalar_tensor_tensor(
                            out=xm_, in0=xm_, scalar=rs, in1=gfin, op0=ALU.mult, op1=ALU.mult),
                            reads=[rxm, rrs, r_gfin], writes=[rxm])
                        P.op("pool", lambda e, xm_=xm_, tok0=tok0: e.dma_start(out=y_d[tok0:tok0 + 128, :], in_=xm_),
                             reads=[rxm], dma_sem=f"y{st_ * ntile + ti}")

            def interleave3(gens):
                gens = [g for g in gens if g is not None]
                while gens:
                    for g in list(gens):
                        try:
                            next(g)
                        except StopIteration:
                            gens.remove(g)

            interleave3([S1(0)])
            for sg_i in range(NSG):
                interleave3([S2(sg_i), S1(sg_i + 1) if sg_i + 1 < NSG else None])
                S3(sg_i)
            A.release(mseq)

        do_sequence(seqs[1])
        A.release(m0)
        do_sequence(seqs[0])

        block = es.enter_context(nc.Block())
        P.emit(block)
    return nc


def _rope_tables(pos, dim):
    inv = (1.0 / (np.float32(10000.0) ** (np.arange(0, dim, 2, dtype=np.float32) / np.float32(dim)))).astype(np.float32)
    ang = pos.astype(np.float32)[:, None] * inv[None, :]
    return np.cos(ang).astype(np.float32), np.sin(ang).astype(np.float32)


def _core_order(S, half):
    NQ = S // 2
    if half == 0:
        own = np.arange(0, NQ)
        halo = np.arange(NQ, NQ + 128)
        rest = np.arange(NQ + 128, S)
    else:
        own = np.arange(NQ, S)
        halo = np.arange(NQ - 128, NQ)
        rest = np.arange(0, NQ - 128)
    return np.concatenate([halo, own, rest]), own


_PROG_CACHE = {}


def _prepare(x_prompt, x_sample, g_mix, w_in, sink, cq_g, w_uq, ckv_g, w_ukv, w_o, g_ffn,
           w_gate, w_up, w_down, g_final):
    f = lambda a: np.ascontiguousarray(np.asarray(a, dtype=np.float32))
    x_prompt, x_sample = f(x_prompt), f(x_sample)
    SP, SS = x_prompt.shape[1], x_sample.shape[1]
    n_cores = 8
    key = (SP, SS)
    if key not in _PROG_CACHE:
        _PROG_CACHE[key] = build_program(SP, SS)
    nc = _PROG_CACHE[key]

    w_in0, w_uq0, w_ukv0, w_o0 = f(w_in)[0], f(w_uq)[0], f(w_ukv)[0], f(w_o)[0]
    w_gate0, w_up0, w_down0 = f(w_gate)[0], f(w_up)[0], f(w_down)[0]
    cols = []
    for h in range(HB):
        b = h * 96 + 64
        cols += list(range(b + 16, b + 32)) + list(range(b, b + 16))
    w_uqs0 = w_uq0[:, cols]
    w_uqc0 = np.ascontiguousarray(np.concatenate(
        [np.concatenate([w_uq0[:, h * 96:(h + 1) * 96], w_uqs0[:, h * 32:(h + 1) * 32]], axis=1) for h in range(HB)],
        axis=1))
    gcols = np.zeros((128, 21), np.float32)
    gcols[:, 0:8] = f(g_mix)[0].reshape(8, 128).T
    gcols[:, 8:16] = f(g_ffn)[0].reshape(8, 128).T
    gcols[:, 16:19] = f(cq_g)[0].reshape(3, 128).T
    gcols[:, 19:21] = f(ckv_g)[0].reshape(2, 128).T
    jj = np.arange(128)[:, None]
    ii = np.arange(128)[None, :]
    maskP = (jj >= ii).astype(np.float32)
    maskN = (jj <= ii).astype(np.float32)
    zero = np.zeros((128, 128), np.float32)

    in_maps = []
    owns = []
    for c in range(n_cores):
        b, half = c // 2, c % 2
        m = {}
        for tag, xs in (("p", x_prompt), ("s", x_sample)):
            S = xs.shape[1]
            NQ = S // 2
            order, own = _core_order(S, half)
            m["x" + tag] = np.ascontiguousarray(xs[b][order])
            cA, sA = _rope_tables(order[:NQ + 128], DA)
            m["ropeA_" + tag] = np.ascontiguousarray(np.concatenate([cA, sA], axis=1))
            cR, sR = _rope_tables(order, DR)
            m["ropeR_" + tag] = np.ascontiguousarray(np.concatenate([cR, sR], axis=1))
            cq, sq_ = _rope_tables(own, DR)
            m["qrc_" + tag] = np.ascontiguousarray(np.concatenate([cq, cq], axis=1).T)
            m["qrs_" + tag] = np.ascontiguousarray(np.concatenate([-sq_, sq_], axis=1).T)
        m["w_in"], m["w_uqc"], m["w_ukv"], m["w_o"] = w_in0, w_uqc0, w_ukv0, w_o0
        m["w_gate"], m["w_up"], m["w_down"] = w_gate0, w_up0, w_down0
        m["gcols"] = gcols
        m["g_final"] = f(g_final)
        m["sink"] = f(sink)[0]
        mk = np.stack([maskP, maskN, zero if half == 0 else maskP, maskN if half == 0 else zero], axis=1)
        m["masks"] = np.ascontiguousarray(mk)
        in_maps.append(m)

    return nc, in_maps, x_prompt, x_sample


def kernel(x_prompt, x_sample, g_mix, w_in, sink, cq_g, w_uq, ckv_g, w_ukv, w_o, g_ffn,
           w_gate, w_up, w_down, g_final):
    nc, in_maps, x_prompt, x_sample = _prepare(x_prompt, x_sample, g_mix, w_in, sink, cq_g, w_uq, ckv_g, w_ukv,
                                                w_o, g_ffn, w_gate, w_up, w_down, g_final)
    SP, SS = x_prompt.shape[1], x_sample.shape[1]
    n_cores = 8
    res = run_bass_kernel_spmd(nc, in_maps, core_ids=list(range(n_cores)))
    yp = np.empty_like(x_prompt)
    ys = np.empty_like(x_sample)
    for c in range(n_cores):
        b, half = c // 2, c % 2
        r = res.results[c]
        NQp, NQs = SP // 2, SS // 2
        yp[b, half * NQp:(half + 1) * NQp] = r["yp"]
        ys[b, half * NQs:(half + 1) * NQs] = r["ys"]
    return (yp, ys)
```

```python
import os
import numpy as np
from contextlib import ExitStack
import concourse.bass as bass
import concourse.mybir as mybir
from concourse.bass_utils import run_bass_kernel_spmd

F32 = mybir.dt.float32
BF16 = mybir.dt.bfloat16
AF = mybir.ActivationFunctionType
ALU = mybir.AluOpType

D = 1024
DIN = 1440
DFF = 2816
NJ = DFF // 128
EPS = 1e-6
HA, KVA, DA = 8, 2, 64
HB, QR, KVR, DN, DR, DV = 8, 384, 256, 64, 32, 64
SC_A = DA ** -0.5
SC_B = (DN + DR) ** -0.5
C_QA, C_CQ, C_KA, C_VA, C_CKV, C_KR = 0, 512, 896, 1024, 1152, 1408


class Res:
    __slots__ = ("name", "w", "r")

    def __init__(self, name):
        self.name = name
        self.w = {}
        self.r = {}


class Op:
    __slots__ = ("eng", "fn", "deps", "need_inc", "inc_val", "dma_sem", "dma_val", "key", "seq")


COMPUTE = ("pe", "act", "dve", "pool")


class Prog:
    def __init__(self, nc, es):
        self.nc = nc
        self.es = es
        self.ops = {e: [] for e in COMPUTE + ("sp",)}
        self.dma_sems = {}
        self.pending = {e: [] for e in COMPUTE + ("sp",)}
        self.sems = {e: es.enter_context(nc.semaphore("s_" + e)) for e in COMPUTE}
        self.uq = 0

    def uniq(self):
        self.uq += 1
        return f"u{self.uq}"

    def _key(self, o):
        return o.eng if o.dma_sem is None else ("dma", id(o.dma_sem))

    def op(self, eng, fn, reads=(), writes=(), dma_sem=None):
        o = Op()
        self.uq += 1
        o.seq = self.uq
        o.eng, o.fn, o.need_inc, o.inc_val = eng, fn, False, None
        o.dma_sem, o.dma_val = None, None
        if dma_sem is not None:
            ds = self.dma_sems.get(dma_sem)
            if ds is None:
                ds = [self.es.enter_context(self.nc.semaphore("dq_" + dma_sem)), 0]
                self.dma_sems[dma_sem] = ds
            ds[1] += 16
            o.dma_sem, o.dma_val = ds, ds[1]
        o.key = self._key(o)
        excl = [r for r in reads if r.name.startswith("bank")]
        if excl:
            writes = list(writes) + [r for r in excl if r not in writes]
            reads = [r for r in reads if not r.name.startswith("bank")]
        deps = {}

        def add(p, kind):
            if p is o:
                return
            if p.eng == eng and eng == "pe" and p.dma_sem is None:
                return
            deps[id(p)] = p

        for r in reads:
            for p in r.w.values():
                add(p, "raw")
        for w in writes:
            for p in w.w.values():
                add(p, "waw")
            for p in w.r.values():
                add(p, "war")
        for p in self.pending[eng]:
            deps[id(p)] = p
        self.pending[eng] = []
        for r in reads:
            r.r[o.key] = o
        for w in writes:
            w.w[o.key] = o
            w.r = {}
        o.deps = list(deps.values())
        for p in o.deps:
            p.need_inc = True
        self.ops[eng].append(o)
        return o

    def barrier(self):
        last = [self.ops[e][-1] for e in self.ops if self.ops[e]]
        seen = {}
        for e_ in self.ops:
            for o in self.ops[e_]:
                if o.dma_sem is not None:
                    seen[id(o.dma_sem)] = o
        last = [o for o in last if o.dma_sem is None] + list(seen.values())
        for e in self.pending:
            self.pending[e] = [p for p in last]

    def emit(self, block):
        nc = self.nc
        sems = self.sems
        for e in COMPUTE:
            n = 0
            for o in self.ops[e]:
                if o.need_inc and o.dma_sem is None:
                    n += 1
                    o.inc_val = n

        def run(eng_name, eng):
            waited = {}
            for o in self.ops[eng_name]:
                for p in o.deps:
                    if p.dma_sem is not None:
                        s, v = p.dma_sem[0], p.dma_val
                    else:
                        s, v = sems[p.eng], p.inc_val
                    k = id(s)
                    if waited.get(k, 0) >= v:
                        continue
                    waited[k] = v
                    eng.wait_ge(s, v)
                ins = o.fn(eng)
                if o.dma_sem is not None:
                    ins.then_inc(o.dma_sem[0], 16)
                elif o.need_inc:
                    ins.then_inc(sems[eng_name], 1)
            if eng_name == "sp":
                for ds in self.dma_sems.values():
                    if waited.get(id(ds[0]), 0) < ds[1]:
                        eng.wait_ge(ds[0], ds[1])

        @block.sync
        def _(e):
            run("sp", e)

        @block.tensor
        def _(e):
            run("pe", e)

        @block.scalar
        def _(e):
            run("act", e)

        @block.vector
        def _(e):
            run("dve", e)

        @block.gpsimd
        def _(e):
            run("pool", e)


class Arena:
    def __init__(self, ap, nbytes):
        self.ap = ap
        self.nbytes = nbytes
        self.top = 0
        self.peak = 0
        self.hist = []

    def _inherit(self, res, s0, e0):
        for (s1, e1, r1) in self.hist:
            if s1 < e0 and s0 < e1:
                for k, o in r1.w.items():
                    if k not in res.w or res.w[k].seq < o.seq:
                        res.w[k] = o
                for k, o in r1.r.items():
                    if k not in res.r or res.r[k].seq < o.seq:
                        res.r[k] = o

    def register(self, owner_res, extra_res):
        for (s1, e1, r1) in list(self.hist):
            if r1 is owner_res:
                self._inherit(extra_res, s1, e1)
                self.hist.append((s1, e1, extra_res))

    def alloc(self, name, free_shape, dtype, slots=1):
        esz = 4 if dtype == F32 else 2
        n = int(np.prod(free_shape))
        out = []
        for s in range(slots):
            off = (self.top + 63) // 64 * 64
            self.top = off + n * esz
            assert self.top <= self.nbytes, f"SBUF arena overflow at {name}: {self.top} > {self.nbytes}"
            v = self.ap[:, off // 2: off // 2 + n * esz // 2]
            if dtype == F32:
                v = v.bitcast(F32)
            if len(free_shape) == 2:
                v = v.rearrange("p (a b) -> p a b", b=free_shape[1])
            elif len(free_shape) == 3:
                v = v.rearrange("p (a b c) -> p a b c", b=free_shape[1], c=free_shape[2])
            res_ = Res(f"{name}{s}")
            self._inherit(res_, off, self.top)
            self.hist.append((off, self.top, res_))
            out.append((v, res_))
        self.peak = max(self.peak, self.top)
        return out

    def mark(self):
        return self.top

    def release(self, m):
        self.top = m


def build_program(SP, SS, TFFN=512):
    seqs = [dict(tag="p", S=SP, NQ=SP // 2), dict(tag="s", S=SS, NQ=SS // 2)]
    nc = bass.Bass("TRN2", target_bir_lowering=False)

    def din(name, shape, dt=F32):
        return nc.dram_tensor(name, list(shape), dt, kind="ExternalInput").ap()

    def dscr(name, shape, dt=BF16):
        return nc.dram_tensor(name, list(shape), dt, kind="Internal").ap()

    for sq in seqs:
        t, S, NQ = sq["tag"], sq["S"], sq["NQ"]
        sq["x"] = din("x" + t, [S, D])
        sq["ropeA"] = din("ropeA_" + t, [NQ + 128, 64])
        sq["ropeR"] = din("ropeR_" + t, [S, 32])
        sq["qrc"] = din("qrc_" + t, [32, NQ])
        sq["qrs"] = din("qrs_" + t, [32, NQ])
        sq["y"] = nc.dram_tensor("y" + t, [NQ, D], F32, kind="ExternalOutput").ap()
    w_in = din("w_in", [D, DIN])
    w_uqc = din("w_uqc", [QR, 1024])
    w_ukv = din("w_ukv", [KVR, 1024])
    w_o = din("w_o", [D, D])
    w_gate = din("w_gate", [D, DFF])
    w_up = din("w_up", [D, DFF])
    w_down = din("w_down", [DFF, D])
    gcols = din("gcols", [128, 21])
    g_final = din("g_final", [D])
    sink = din("sink", [8])
    masks_d = din("masks", [128, 4, 128])
    s_win = dscr("s_win", [128, 8 * DIN])
    s_wuq = dscr("s_wuq", [128, 3 * 1024])
    s_wukv = dscr("s_wukv", [128, 2 * 1024])
    s_wo = dscr("s_wo", [128, 8 * D])
    s_wg = dscr("s_wg", [NJ, 128, D])
    s_wu = dscr("s_wu", [NJ, 128, D])
    s_wd = dscr("s_wd", [NJ, 128, D])

    es = ExitStack()
    with es:
        ARENA_BYTES = 212736
        arena_t = es.enter_context(nc.sbuf_tensor("arena", [128, ARENA_BYTES // 2], BF16))
        ps_t = es.enter_context(nc.psum_tensor("ps", [128, 4096], F32))
        P = Prog(nc, es)
        A = Arena(arena_t[:, :], ARENA_BYTES)
        bankres = [Res(f"bank{b}") for b in range(8)]

        def bank(b):
            return ps_t[:, b * 512:(b + 1) * 512]

        def bank_bf(b):
            return ps_t[:, b * 512:(b + 1) * 512].bitcast(BF16)

        (ident, r_ident), = A.alloc("ident", [128], BF16)
        (esink, r_esink), = A.alloc("esink", [8], F32)
        (esk, r_esk), = A.alloc("esk", [8, 128], BF16)
        (sel1, r_sel1), = A.alloc("sel1", [128], BF16)
        (gc, r_gc), = A.alloc("gc", [21], F32)
        (stat, _), = A.alloc("stat", [64], F32)
        (epst, r_epst), = A.alloc("epst", [1], F32)
        stat_res = [Res(f"stat{i}") for i in range(64)]

        def st(i):
            return stat[:, i:i + 1], stat_res[i]

        P.op("pool", lambda e: e.memset(ident, 0.0), writes=[r_ident])
        P.op("pool", lambda e: e.affine_select(out=ident, in_=ident, pattern=[[-1, 128]],
                                               compare_op=ALU.not_equal, fill=1.0, base=0,
                                               channel_multiplier=1), reads=[r_ident], writes=[r_ident])
        P.op("pool", lambda e: e.memset(epst, EPS), writes=[r_epst])
        P.op("pool", lambda e: e.memset(sel1[0:1, 0:64], 0.0), writes=[r_sel1])
        P.op("pool", lambda e: e.memset(sel1[0:1, 64:128], 1.0), writes=[r_sel1])
        P.op("sp", lambda e: e.dma_start(out=gc, in_=gcols[:, :]), writes=[r_gc], dma_sem=P.uniq())
        P.op("sp", lambda e: e.dma_start(out=esink[0:1, :], in_=sink.partition_broadcast(1)),
             writes=[r_esink], dma_sem=P.uniq())
        P.op("act", lambda e: e.activation(out=esink[0:1, :], in_=esink[0:1, :], func=AF.Exp),
             reads=[r_esink], writes=[r_esink])
        P.op("dve", lambda e: e.tensor_copy(out=esk[0:1, :, :],
                                            in_=esink[0:1, :].unsqueeze(2).to_broadcast([1, 8, 128])),
             reads=[r_esink], writes=[r_esk])

        m0 = A.mark()
        stg_f = A.alloc("stg_f", [DFF], F32, slots=2)
        stg_b = A.alloc("stg_b", [DFF], BF16, slots=2)
        cnt = [0]

        scr = {}

        def convert(src_ap, ncols, gcol, dst_ap, perm=None, tag="x"):
            k = cnt[0]
            rs_ = Res(f"scr_{tag}{k}")
            scr.setdefault(tag, []).append(rs_)
            cnt[0] += 1
            (sf, rf), (sb, rb) = stg_f[k % 2], stg_b[k % 2]
            P.op("sp", lambda e: e.dma_start(out=sf[:, 0:ncols], in_=src_ap), writes=[rf], dma_sem=f"p0l{k % 2}")
            eng = "dve" if (k % 2 == 0 or tag != "win") else "act"
            pieces = perm if perm is not None else [(sb[:, 0:ncols], sf[:, 0:ncols])]
            for (o_ap, i_ap) in pieces:
                if gcol is None:
                    if eng == "dve":
                        P.op("dve", lambda e, o_ap=o_ap, i_ap=i_ap: e.tensor_copy(out=o_ap, in_=i_ap),
                             reads=[rf], writes=[rb])
                    else:
                        P.op("act", lambda e, o_ap=o_ap, i_ap=i_ap: e.copy(out=o_ap, in_=i_ap),
                             reads=[rf], writes=[rb])
                else:
                    if eng == "dve":
                        P.op("dve", lambda e, o_ap=o_ap, i_ap=i_ap: e.tensor_scalar(
                            out=o_ap, in0=i_ap, scalar1=gcol, scalar2=None, op0=ALU.mult),
                            reads=[rf, r_gc], writes=[rb])
                    else:
                        P.op("act", lambda e, o_ap=o_ap, i_ap=i_ap: e.activation(
                            out=o_ap, in_=i_ap, func=AF.Copy, scale=gcol),
                            reads=[rf, r_gc], writes=[rb])
            P.op("sp", lambda e: e.dma_start(out=dst_ap, in_=sb[:, 0:ncols] if dst_ap.ndim == 2 else
                                             sb[:, 0:ncols].rearrange("p (j f) -> p j f", f=128)),
                 reads=[rb], writes=[rs_], dma_sem=f"p0s{k % 2}")

        for c in range(8):
            k = cnt[0]
            sf, sb = stg_f[k % 2][0], stg_b[k % 2][0]
            perm = [
                (sb[:, 0:512].rearrange("p (h g d) -> p g h d", h=4, g=2),
                 sf[:, 0:512].rearrange("p (g h d) -> p g h d", g=2, h=4)),
                (sb[:, C_CQ:C_CQ + 384], sf[:, 768:1152]),
                (sb[:, C_KA:C_KA + 256], sf[:, 512:768]),
                (sb[:, C_CKV:C_CKV + 288], sf[:, 1152:1440]),
            ]
            convert(w_in[c * 128:(c + 1) * 128, :], DIN, gc[:, c:c + 1], s_win[:, c * DIN:(c + 1) * DIN], perm, tag="win")
        def conv_rest():
            for c in range(3):
                convert(w_uqc[c * 128:(c + 1) * 128, :], 1024, gc[:, 16 + c:17 + c],
                        s_wuq[:, c * 1024:(c + 1) * 1024], tag="wuq")
                yield
            for c in range(2):
                convert(w_ukv[c * 128:(c + 1) * 128, :], 1024, gc[:, 19 + c:20 + c],
                        s_wukv[:, c * 1024:(c + 1) * 1024], tag="wukv")
                yield
            for c in range(8):
                convert(w_o[c * 128:(c + 1) * 128, :], D, None, s_wo[:, c * D:(c + 1) * D], tag="wo")
                yield
            for (wsrc, sdst, tg) in ((w_gate, s_wg, "wg"), (w_up, s_wu, "wu")):
                dv = sdst.rearrange("j p (c f) -> c p j f", c=8)
                for c in range(8):
                    convert(wsrc[c * 128:(c + 1) * 128, :], DFF, gc[:, 8 + c:9 + c], dv[c], tag=tg)
                    yield
            for j in range(NJ):
                convert(w_down[j * 128:(j + 1) * 128, :], D, None, s_wd[j], tag="wd")
                yield

        bg = conv_rest()

        def do_sequence(sq):
            S, NQ = sq["S"], sq["NQ"]
            nb, nt = NQ // 128, S // 128
            QB = min(512, NQ)
            nqb = NQ // QB
            x_d, y_d = sq["x"], sq["y"]
            mseq = A.mark()
            (ocat, r_ocat), = A.alloc("ocat", [8, NQ], BF16)
            m_ocat = A.mark()
            (ckvT, r_ckvT), = A.alloc("ckvT", [2, S], BF16)
            (KT, r_KT), = A.alloc("KT", [S], BF16)
            r_KTr = Res("KTr")
            A.register(r_KT, r_KTr)
            (cqT, r_cqT), = A.alloc("cqT", [3, NQ], BF16)

            NXT = 2
            m1 = A.mark()
            (win, r_win), = A.alloc("win", [8, DIN], BF16)
            (msk, r_msk), = A.alloc("msk", [4, 512], BF16)
            xt = A.alloc("xt", [D], F32, slots=NXT)
            (junk, r_junk), = A.alloc("junk", [D], BF16)
            xn = A.alloc("xn", [D], BF16, slots=2)
            xnT = A.alloc("xnT", [8, 128], BF16, slots=2)
            ckvn = A.alloc("ckvn", [256], BF16, slots=2)
            krst = A.alloc("krst", [128], BF16, slots=2)
            rA = A.alloc("rA", [64], F32, slots=2)
            rR = A.alloc("rR", [32], F32, slots=2)
            (tAq, r_tAq), = A.alloc("tAq", [512], F32)
            mskf, r_mskf = tAq.rearrange("p (a b) -> p a b", a=4), r_tAq
            (tBq, r_tBq), = A.alloc("tBq", [512], F32)
            (tAk, r_tAk), = A.alloc("tAk", [128], F32)
            (tBk, r_tBk), = A.alloc("tBk", [128], F32)
            (tAr, r_tAr), = A.alloc("tAr", [32], F32)
            (tBr, r_tBr), = A.alloc("tBr", [32], F32)
            kab = A.alloc("kab", [128], BF16, slots=2)
            qab = A.alloc("qab", [512], BF16, slots=2)
            cqn = A.alloc("cqn", [384], BF16, slots=2)
            KAT = A.alloc("KAT", [128], BF16, slots=5)
            VA = A.alloc("VA", [2, 128], BF16, slots=5)
            QAT = A.alloc("QAT", [4, 128], BF16, slots=3)
            PTw = A.alloc("PTw", [512], BF16, slots=3)
            (densb, r_densb), = A.alloc("densb", [512], F32)
            rden, r_rden = densb, r_densb

            P.op("sp", lambda e: e.dma_start(out=win, in_=s_win.rearrange("p (c n) -> p c n", c=8)),
                 reads=scr["win"], writes=[r_win], dma_sem=P.uniq())
            P.op("sp", lambda e: e.dma_start(out=mskf, in_=masks_d), writes=[r_mskf], dma_sem=P.uniq())
            for k_ in range(4):
                P.op("dve", lambda e, k_=k_: e.tensor_scalar(
                    out=msk[:, k_, :].rearrange("p (a b) -> p a b", a=4),
                    in0=mskf[:, k_, :].unsqueeze(1).to_broadcast([128, 4, 128]),
                    scalar1=-1.0, scalar2=30000.0, op0=ALU.add, op1=ALU.mult), reads=[r_mskf], writes=[r_msk])
            for s_ in range(2):
                P.op("pool", lambda e, s_=s_: e.memset(krst[s_][0], 0.0), writes=[krst[s_][1]])
            for s_ in range(5):
                P.op("pool", lambda e, s_=s_: e.memset(VA[s_][0], 1.0), writes=[VA[s_][1]])

            def rstd_chain(src_ap, src_res, ncols, base, extra_reads=()):
                (a0, r0), (a1, r1), (a3, r3) = st(base), st(base + 1), st(base + 3)
                P.op("act", lambda e: e.activation(out=junk[:, 0:ncols], in_=src_ap, func=AF.Square, accum_out=a0),
                     reads=[src_res], writes=[r_junk, r0])
                P.op("act", lambda e: e.activation(out=a1, in_=a0, func=AF.Ln, scale=1.0 / ncols, bias=epst),
                     reads=[r0, r_epst], writes=[r1])
                P.op("act", lambda e: e.activation(out=a3, in_=a1, func=AF.Exp, scale=-0.5), reads=[r1], writes=[r3])
                return a3, r3

            def rope_tok(src4, src_res, cos_b, sin_b, tab_res, tA, r_tA, tB, r_tB, dst4, dst_res, shp):
                tA4 = tA.rearrange("p (a b c) -> p a b c", b=2, c=shp[2]) if tA.ndim == 2 else tA
                tB4 = tB.rearrange("p (a b c) -> p a b c", b=2, c=shp[2]) if tB.ndim == 2 else tB
                P.op("dve", lambda e: e.tensor_tensor(out=tA4, in0=src4, in1=cos_b, op=ALU.mult),
                     reads=[src_res, tab_res], writes=[r_tA])
                P.op("dve", lambda e: e.tensor_tensor(out=tB4, in0=src4, in1=sin_b, op=ALU.mult),
                     reads=[src_res, tab_res], writes=[r_tB])
                P.op("dve", lambda e: e.tensor_tensor(out=dst4[:, :, 0, :], in0=tA4[:, :, 0, :], in1=tB4[:, :, 1, :],
                                                      op=ALU.subtract), reads=[r_tA, r_tB], writes=[dst_res])
                P.op("dve", lambda e: e.tensor_tensor(out=dst4[:, :, 1, :], in0=tA4[:, :, 1, :], in1=tB4[:, :, 0, :],
                                                      op=ALU.add), reads=[r_tA, r_tB], writes=[dst_res])

            def slot_of(tile):
                return 4 if tile == 0 else (tile % 4)

            def stageA(i):
                (x_, rx), (xn_, rxn), (xT_, rxT) = xt[i % NXT], xn[i % 2], xnT[i % 2]
                P.op("sp", lambda e: e.dma_start(out=x_, in_=x_d[i * 128:(i + 1) * 128, :]), writes=[rx],
                     dma_sem=f"xt{i % NXT}")
                rs, rrs = rstd_chain(x_, rx, D, (i % 2) * 4)
                yield
                P.op("dve", lambda e: e.tensor_scalar(out=xn_, in0=x_, scalar1=rs, scalar2=None, op0=ALU.mult),
                     reads=[rx, rrs], writes=[rxn])
                yield

                def tr(e):
                    pb = bank_bf(0)
                    for c in range(8):
                        ins = e.transpose(out=pb[:, c * 128:(c + 1) * 128], in_=xn_[:, c * 128:(c + 1) * 128],
                                          identity=ident)
                    return ins
                P.op("pe", tr, reads=[rxn, r_ident], writes=[bankres[0]])
                yield
                P.op("act", lambda e: e.copy(out=xT_, in_=bank_bf(0).rearrange("p (c t) -> p c t", c=8)),
                     reads=[bankres[0]], writes=[rxT])
                yield

            def mm_group(e, out_ap, xT_, c0, ncols):
                for c in range(8):
                    ins = e.matmul(out_ap, lhsT=xT_[:, c, :], rhs=win[:, c, c0:c0 + ncols],
                                   start=(c == 0), stop=(c == 7))
                return ins

            def stageB(i):
                own = 1 <= i <= nb
                hown = i <= nb
                (xT_, rxT) = xnT[i % 2]
                tok0 = i * 128
                psT8 = bank_bf(4).rearrange("p (a t) -> p a t", a=8)
                psTa = psT8[:, 0:4, :]
                psTb = psT8
                if hown:
                    P.op("pe", lambda e: mm_group(e, bank(3)[:, 0:416], xT_, C_VA, 416), reads=[rxT, r_win],
                         writes=[bankres[3]])
                    P.op("pe", lambda e: mm_group(e, bank(2)[:, 0:512], xT_, C_CQ, 512), reads=[rxT, r_win],
                         writes=[bankres[2]])
                else:
                    P.op("pe", lambda e: mm_group(e, bank(3)[:, 128:416], xT_, C_CKV, 288), reads=[rxT, r_win],
                         writes=[bankres[3]])
                if own:
                    P.op("pe", lambda e: mm_group(e, bank(1)[:, 0:512], xT_, C_QA, 512), reads=[rxT, r_win],
                         writes=[bankres[1]])
                yield
                (ck_, rck), (kr_, rkr) = ckvn[i % 2], krst[i % 2]
                rs, rrs = rstd_chain(bank(3)[:, 128:384], bankres[3], 256, 8 + (i % 2) * 4)
                P.op("dve", lambda e: e.tensor_scalar(out=ck_, in0=bank(3)[:, 128:384], scalar1=rs, scalar2=None,
                                                      op0=ALU.mult), reads=[bankres[3], rrs], writes=[rck])
                yield
                (rr_, rrr) = rR[i % 2]
                P.op("pool", lambda e: e.dma_start(out=rr_, in_=sq["ropeR"][tok0:tok0 + 128, :]), writes=[rrr],
                     dma_sem=f"rR{i % 2}")
                src4 = bank(3)[:, 384:416].rearrange("p (a b c) -> p a b c", a=1, b=2)
                cos_b = rr_[:, 0:16].unsqueeze(1).unsqueeze(1).to_broadcast([128, 1, 2, 16])
                sin_b = rr_[:, 16:32].unsqueeze(1).unsqueeze(1).to_broadcast([128, 1, 2, 16])
                dst4 = kr_[:, 64:96].rearrange("p (a b c) -> p a b c", a=1, b=2)
                rope_tok(src4, bankres[3], cos_b, sin_b, rrr, tAr, r_tAr, tBr, r_tBr, dst4, rkr, (1, 2, 16))

                yield
                def trB(e):
                    e.transpose(out=psTa[:, 0, :], in_=ck_[:, 0:128], identity=ident)
                    e.transpose(out=psTa[:, 1, :], in_=ck_[:, 128:256], identity=ident)
                    return e.transpose(out=psTa[:, 2, :], in_=kr_[:, 0:128], identity=ident)
                P.op("pe", trB, reads=[rck, rkr, r_ident], writes=[bankres[4]])
                yield
                P.op("act", lambda e: e.copy(out=ckvT[:, :, tok0:tok0 + 128], in_=psTa[:, 0:2, :]),
                     reads=[bankres[4]], writes=[r_ckvT])
                P.op("act", lambda e: e.copy(out=KT[64:96, tok0:tok0 + 128], in_=psTa[64:96, 2, :]),
                     reads=[bankres[4]], writes=[r_KTr])
                yield
                if hown:
                    sl = slot_of(i)
                    (ra_, rra) = rA[i % 2]
                    P.op("pool", lambda e: e.dma_start(out=ra_, in_=sq["ropeA"][tok0:tok0 + 128, :]), writes=[rra],
                         dma_sem=f"rA{i % 2}")
                    (kab_, rkab) = kab[i % 2]
                    src4 = bank(2)[:, 384:512].rearrange("p (a b c) -> p a b c", a=2, b=2)
                    cos_b = ra_[:, 0:32].unsqueeze(1).unsqueeze(1).to_broadcast([128, 2, 2, 32])
                    sin_b = ra_[:, 32:64].unsqueeze(1).unsqueeze(1).to_broadcast([128, 2, 2, 32])
                    dst4 = kab_.rearrange("p (a b c) -> p a b c", a=2, b=2)
                    rope_tok(src4, bankres[2], cos_b, sin_b, rra, tAk, r_tAk, tBk, r_tBk, dst4, rkab, (2, 2, 32))
                    yield
                    (va_, rva), (kat_, rkat) = VA[sl], KAT[sl]
                    P.op("act", lambda e: e.copy(out=va_[:, :, 0:64],
                                                 in_=bank(3)[:, 0:128].rearrange("p (a b) -> p a b", a=2)),
                         reads=[bankres[3]], writes=[rva])
                    P.op("pe", lambda e: e.transpose(out=psTa[:, 3, :], in_=kab_, identity=ident),
                         reads=[rkab, r_ident], writes=[bankres[4]])
                    P.op("act", lambda e: e.copy(out=kat_, in_=psTa[:, 3, :]), reads=[bankres[4]], writes=[rkat])
                yield
                if own:
                    (qab_, rqab) = qab[i % 2]
                    (ra_, rra) = rA[i % 2]
                    src4 = bank(1)[:, 0:512].rearrange("p (a b c) -> p a b c", a=8, b=2)
                    cos_b = ra_[:, 0:32].unsqueeze(1).unsqueeze(1).to_broadcast([128, 8, 2, 32])
                    sin_b = ra_[:, 32:64].unsqueeze(1).unsqueeze(1).to_broadcast([128, 8, 2, 32])
                    dst4 = qab_.rearrange("p (a b c) -> p a b c", a=8, b=2)
                    rope_tok(src4, bankres[1], cos_b, sin_b, rra, tAq, r_tAq, tBq, r_tBq, dst4, rqab, (8, 2, 32))

                    def trQ(e):
                        for h in range(4):
                            ins = e.transpose(out=psTb[:, 4 + h, :], in_=qab_[:, h * 128:(h + 1) * 128], identity=ident)
                        return ins
                    yield
                    P.op("pe", trQ, reads=[rqab, r_ident], writes=[bankres[4]])
                    (qat_, rqat) = QAT[i % 3]
                    P.op("act", lambda e: e.copy(out=qat_, in_=psTb[:, 4:8, :]), reads=[bankres[4]], writes=[rqat])
                    yield
                    (cq_, rcq) = cqn[i % 2]
                    rs2, rrs2 = rstd_chain(bank(2)[:, 0:384], bankres[2], 384, 16 + (i % 2) * 4)
                    P.op("dve", lambda e: e.tensor_scalar(out=cq_, in0=bank(2)[:, 0:384], scalar1=rs2, scalar2=None,
                                                          op0=ALU.mult), reads=[bankres[2], rrs2], writes=[rcq])

                    def trC(e):
                        for c in range(3):
                            ins = e.transpose(out=psTb[:, 4 + c, :], in_=cq_[:, c * 128:(c + 1) * 128],
                                              identity=ident)
                        return ins
                    yield
                    P.op("pe", trC, reads=[rcq, r_ident], writes=[bankres[4]])
                    q0 = (i - 1) * 128
                    P.op("act", lambda e: e.copy(out=cqT[:, :, q0:q0 + 128], in_=psTb[:, 4:7, :]),
                         reads=[bankres[4]], writes=[r_cqT])

            def stageC(b):
                cur = b + 1
                prev = b
                nxt = b + 2 if b < nb - 1 else 0
                pm = 2 if b == 0 else 0
                nm = 3 if b == nb - 1 else 1
                (qat_, rqat) = QAT[cur % 3]
                tok0 = b * 128

                def unit_qk(kv, n_, tile, mk):
                    rows = slice(kv * 64, (kv + 1) * 64)
                    sl = slot_of(tile)
                    (kat_, rkat) = KAT[sl]
                    sb_ = 5 + (kv * 3 + n_) % 2

                    def qkm(e):
                        ins = e.matmul(bank(sb_), lhsT=kat_[rows, :], rhs=qat_[rows, :, :], start=True,
                                       stop=(mk is None))
                        if mk is not None:
                            ins = e.matmul(bank(sb_), lhsT=ident, rhs=msk[:, mk, :], start=False, stop=True)
                        return ins
                    P.op("pe", qkm, reads=[rkat, rqat, r_msk, r_ident], writes=[bankres[sb_]])

                def unit_pv(kv, n_, tile, mk):
                    sl = slot_of(tile)
                    (va_, rva) = VA[sl]
                    (pt_, rpt) = PTw[(b * 6 + kv * 3 + n_) % 3]
                    sb_ = 5 + (kv * 3 + n_) % 2
                    P.op("act", lambda e: e.activation(out=pt_, in_=bank(sb_), func=AF.Exp, scale=SC_A),
                         reads=[bankres[sb_]], writes=[rpt])

                    def pvm(e):
                        ins = e.matmul(bank(7), lhsT=va_[:, kv, :], rhs=pt_, start=(n_ == 0), stop=False)
                        if n_ == 2:
                            ins = e.matmul(bank(7), lhsT=sel1[0:1, :],
                                           rhs=esk[0:1, kv * 4:(kv + 1) * 4, :], start=False, stop=True)
                        return ins
                    P.op("pe", pvm, reads=[rva, rpt, r_sel1, r_esk], writes=[bankres[7]])

                def epi(kv):
                    P.op("act", lambda e: e.activation(out=densb[0:64, :], in_=bank(7)[64:128, :], func=AF.Ln),
                         reads=[bankres[7]], writes=[r_densb])
                    P.op("act", lambda e: e.activation(out=densb[0:64, :], in_=densb[0:64, :], func=AF.Exp, scale=-1.0),
                         reads=[r_densb], writes=[r_densb])
                    o4 = bank(7)[0:64, :].rearrange("p (a b) -> p a b", a=4)
                    b4 = densb[0:64, :].rearrange("p (a b) -> p a b", a=4)

                    def fin(half):
                        P.op("dve", lambda e: e.tensor_tensor(
                            out=ocat[half * 64:(half + 1) * 64, kv * 2:kv * 2 + 2, tok0:tok0 + 128],
                            in0=o4[:, half::2, :], in1=b4[:, half::2, :], op=ALU.mult),
                            reads=[bankres[7], r_densb], writes=[r_ocat])
                    fin(0)
                    fin(1)

                units = [(kv, n_, tile, mk) for kv in range(2)
                         for n_, (tile, mk) in enumerate([(prev, pm), (cur, None), (nxt, nm)])]
                unit_qk(*units[0])
                for i_, u_ in enumerate(units):
                    if i_ + 1 < len(units):
                        unit_qk(*units[i_ + 1])
                    unit_pv(*u_)
                    yield
                    if u_[1] == 2:
                        epi(u_[0])
                        yield

            def bgstep(n):
                for _ in range(n):
                    try:
                        next(bg)
                    except StopIteration:
                        return
                    yield

            def interleave(gens):
                gens = [g for g in gens if g is not None]
                while gens:
                    for g in list(gens):
                        try:
                            next(g)
                        except StopIteration:
                            gens.remove(g)

            for step in range(nt + 4):
                gl = []
                if step < nt:
                    gl.append(stageA(step))
                if 1 <= step <= nt:
                    gl.append(stageB(step - 1))
                b = step - 4
                if 0 <= b < nb:
                    gl.append(stageC(b))
                gl.append(bgstep(2))
                interleave(gl)
            for _ in bg:
                pass
            A.release(m1)

            m2 = A.mark()
            (wuq, r_wuq), = A.alloc("wuq", [3, 1024], BF16)
            r_QTq = [Res(f"QTq{q_}") for q_ in range(nqb)]
            (wukv, r_wukv), = A.alloc("wukv", [2, 1024], BF16)
            (Vh, r_Vh), = A.alloc("Vh", [nt, 128], BF16)
            (QT, r_QT), = A.alloc("QT", [NQ], BF16)
            for r__ in r_QTq:
                A.register(r_QT, r__)
            (qrc, r_qrc), = A.alloc("qrc", [NQ], F32)
            qrs, r_qrs = qrc, Res("qrs")
            A.register(r_qrc, r_qrs)
            (t1, r_t1), = A.alloc("t1", [QB], F32)
            (t2, r_t2), = A.alloc("t2", [QB], F32)
            PT = A.alloc("PT", [1024], BF16, slots=3)
            (osb, r_osb), = A.alloc("osb", [QB], F32)
            rden2, r_rden2 = osb, r_osb
            P.op("sp", lambda e: e.dma_start(out=wuq, in_=s_wuq.rearrange("p (c n) -> p c n", c=3)),
                 reads=scr["wuq"], writes=[r_wuq], dma_sem=P.uniq())
            P.op("sp", lambda e: e.dma_start(out=wukv, in_=s_wukv.rearrange("p (c n) -> p c n", c=2)),
                 reads=scr["wukv"], writes=[r_wukv], dma_sem=P.uniq())
            P.op("sp", lambda e: e.dma_start(out=qrc[64:96, :], in_=sq["qrc"][:, :]), writes=[r_qrc], dma_sem=P.uniq())
            P.op("sp", lambda e: e.dma_start(out=qrs[96:128, :], in_=sq["qrs"][:, :]), writes=[r_qrs], dma_sem=P.uniq())
            P.op("pool", lambda e: e.memset(Vh, 1.0), writes=[r_Vh])
            pbc = [0]
            ev = [0]

            def evac(out_ap, in_ap, reads, writes):
                P.op("dve", lambda e: e.tensor_copy(out=out_ap, in_=in_ap), reads=reads, writes=writes)

            pend = []

            def do_head(h):
                KBLK = min(512, S)

                def kgen():
                    for tb in range(S // KBLK):
                        pb = pbc[0] % 6
                        pbc[0] += 1

                        def kp(e, tb=tb, pb=pb):
                            for c in range(2):
                                ins = e.matmul(bank(pb)[0:64, 0:KBLK], lhsT=wukv[:, c, h * 128:h * 128 + 64],
                                               rhs=ckvT[:, c, tb * KBLK:(tb + 1) * KBLK], start=(c == 0), stop=(c == 1))
                            return ins
                        P.op("pe", kp, reads=[r_wukv, r_ckvT], writes=[bankres[pb]])
                        if tb % 2 == 0:
                            P.op("dve", lambda e, tb=tb, pb=pb: e.tensor_copy(
                                out=KT[0:64, tb * KBLK:(tb + 1) * KBLK], in_=bank(pb)[0:64, 0:KBLK]),
                                reads=[bankres[pb]], writes=[r_KT])
                        else:
                            P.op("act", lambda e, tb=tb, pb=pb: e.copy(
                                out=KT[0:64, tb * KBLK:(tb + 1) * KBLK], in_=bank(pb)[0:64, 0:KBLK]),
                                reads=[bankres[pb]], writes=[r_KT])
                        yield

                def vgen():
                    for g in range(nt // 8):
                        pb = pbc[0] % 6
                        pbc[0] += 1

                        def vp(e, g=g, pb=pb):
                            for t_ in range(8):
                                kt = g * 8 + t_
                                for c in range(2):
                                    ins = e.matmul(bank(pb)[:, t_ * 64:(t_ + 1) * 64],
                                                   lhsT=ckvT[:, c, kt * 128:(kt + 1) * 128],
                                                   rhs=wukv[:, c, h * 128 + 64:h * 128 + 128],
                                                   start=(c == 0), stop=(c == 1))
                            return ins
                        P.op("pe", vp, reads=[r_wukv, r_ckvT], writes=[bankres[pb]])
                        P.op("act", lambda e, g=g, pb=pb: e.copy(
                            out=Vh[:, g * 8:(g + 1) * 8, 0:64], in_=bank(pb).rearrange("p (a b) -> p a b", a=8)),
                            reads=[bankres[pb]], writes=[r_Vh])
                        yield
                        yield
                gens_ = [kgen(), vgen()]
                while gens_:
                    for g_ in list(gens_):
                        try:
                            next(g_)
                        except StopIteration:
                            gens_.remove(g_)
                def q_proj(qb, PB):
                    qs = slice(qb * QB, (qb + 1) * QB)

                    def qp(e):
                        for c in range(3):
                            ins = e.matmul(bank(PB)[:, 0:QB], lhsT=wuq[:, c, h * 128:(h + 1) * 128],
                                           rhs=cqT[:, c, qs], start=(c == 0), stop=(c == 2))
                        return ins
                    P.op("pe", qp, reads=[r_wuq, r_cqT], writes=[bankres[PB]])
                    if qb == 0:
                        P.op("act", lambda e: e.copy(out=QT[0:64, qs], in_=bank(PB)[0:64, 0:QB]),
                             reads=[bankres[PB]], writes=[r_QTq[qb]])
                    else:
                        P.op("dve", lambda e: e.tensor_copy(out=QT[0:64, qs], in_=bank(PB)[0:64, 0:QB]),
                             reads=[bankres[PB]], writes=[r_QTq[qb]])
                    P.op("dve", lambda e: e.tensor_tensor(out=t1[64:96, :], in0=bank(PB)[64:96, 0:QB],
                                                          in1=qrc[64:96, qs], op=ALU.mult),
                         reads=[bankres[PB], r_qrc], writes=[r_t1])
                    P.op("dve", lambda e: e.tensor_tensor(out=t2[64:96, :], in0=bank(PB)[96:128, 0:QB],
                                                          in1=qrs[96:128, qs], op=ALU.mult),
                         reads=[bankres[PB], r_qrs], writes=[r_t2])
                    P.op("dve", lambda e: e.tensor_tensor(out=QT[64:96, qs], in0=t1[64:96, :],
                                                          in1=t2[64:96, :], op=ALU.add),
                         reads=[r_t1, r_t2], writes=[r_QTq[qb]])
                q_proj(0, pbc[0] % 6)
                pbc[0] += 1
                npair = nt // 2
                for qb in range(nqb):
                    qs = slice(qb * QB, (qb + 1) * QB)
                    ob = 6 + (qb % 2)

                    def qk(p, qs=qs, qb=qb):
                        b0 = (p % 3) * 2

                        def f(e):
                            for u in range(2):
                                kt = 2 * p + u
                                ins = e.matmul(bank(b0 + u)[:, 0:QB], lhsT=KT[0:96, kt * 128:(kt + 1) * 128],
                                               rhs=QT[0:96, qs], start=True, stop=True)
                            return ins
                        P.op("pe", f, reads=[r_KT, r_KTr, r_QTq[qb]], writes=[bankres[b0], bankres[b0 + 1]])
                        (pt_, rpt) = PT[p % 3]
                        src = ps_t[:, b0 * 512:(b0 + 2) * 512].rearrange("p (a b) -> p a b", a=2)[:, :, 0:QB]
                        P.op("act", lambda e: e.activation(out=pt_.rearrange("p (a b) -> p a b", a=2)[:, :, 0:QB],
                                                           in_=src, func=AF.Exp, scale=SC_B),
                             reads=[bankres[b0], bankres[b0 + 1]], writes=[rpt])

                    def pv(p, ob=ob):
                        (pt_, rpt) = PT[p % 3]

                        def f(e):
                            for u in range(2):
                                kt = 2 * p + u
                                ins = e.matmul(bank(ob)[:, 0:QB], lhsT=Vh[:, kt, :],
                                               rhs=pt_[:, u * 512:u * 512 + QB],
                                               start=(kt == 0), stop=(kt == nt - 1))
                            return ins
                        P.op("pe", f, reads=[r_Vh, rpt], writes=[bankres[ob]])

                    def epilogue(ob=ob, qs=qs, h=h):
                        P.op("dve", lambda e: e.reciprocal(out=osb[0:64, :], in_=bank(ob)[64:128, 0:QB]),
                             reads=[bankres[ob]], writes=[r_osb])
                        half = h % 2
                        P.op("dve", lambda e: e.tensor_tensor(
                            out=ocat[half * 64:(half + 1) * 64, 4 + h // 2, qs], in0=bank(ob)[0:64, 0:QB],
                            in1=osb[0:64, :], op=ALU.mult),
                            reads=[r_osb, bankres[ob]], writes=[r_ocat])

                    for p in range(npair):
                        qk(p)
                        if p == 2 and pend:
                            pend.pop()()
                        if p == min(12, npair - 2) and qb + 1 < nqb:
                            q_proj(qb + 1, 6 + ((qb + 1) % 2))
                        if p >= 2:
                            pv(p - 2)
                    pv(npair - 2)
                    pv(npair - 1)
                    pend.append(epilogue)
                if pend:
                    pend.pop()()
            for h_ in range(HB):
                do_head(h_)
            A.release(m2)
            A.release(m_ocat)
            alloc3 = A.alloc

            T = min(TFFN, NQ)
            ntile = T // 128
            nhalf = max(1, T // 512)
            HW = min(512, T)
            (wo, r_wo), = alloc3("wo", [8, D], BF16)
            (gfin, r_gfin), = alloc3("gfin", [D], F32)
            h2Ts = alloc3("h2T", [8, T], BF16, slots=2)
            (actT, r_actT), = alloc3("actT", [NJ, T], BF16)
            xm = alloc3("xm", [D], F32, slots=2 * ntile)
            (junk3, r_junk3), = alloc3("junk3", [D], BF16)
            h2 = alloc3("h2", [D], BF16, slots=2)
            sg = alloc3("sg", [HW], F32, slots=2)
            wgs = alloc3("wgs", [8, 128], BF16, slots=3)
            wus = alloc3("wus", [8, 128], BF16, slots=3)
            wds = alloc3("wds", [512], BF16, slots=4)
            P.op("sp", lambda e: e.dma_start(out=wo, in_=s_wo.rearrange("p (c n) -> p c n", c=8)),
                 reads=scr["wo"], writes=[r_wo], dma_sem=P.uniq())
            P.op("sp", lambda e: e.dma_start(out=gfin, in_=g_final.partition_broadcast(128)), writes=[r_gfin],
                 dma_sem=P.uniq())
            wcnt = [0, 0]
            NSG = NQ // T

            def S1(sg_i):
                g0 = sg_i * T
                st_ = sg_i % 2
                (h2T_, r_h2T_) = h2Ts[st_]
                sb = 24 + st_ * 12
                for ti in range(ntile):
                    tok0 = g0 + ti * 128
                    (xm_, rxm) = xm[st_ * ntile + ti]
                    P.op("sp", lambda e, xm_=xm_, tok0=tok0: e.dma_start(
                        out=xm_, in_=x_d[128 + tok0:128 + tok0 + 128, :]), writes=[rxm],
                        dma_sem=f"xm{st_ * ntile + ti}")

                    def wo_mm(e, tok0=tok0):
                        for n in range(2):
                            for c in range(8):
                                ins = e.matmul(bank(n), lhsT=ocat[:, c, tok0:tok0 + 128],
                                               rhs=wo[:, c, n * 512:(n + 1) * 512], start=(c == 0), stop=(c == 7))
                        return ins
                    P.op("pe", wo_mm, reads=[r_ocat, r_wo], writes=[bankres[0], bankres[1]])
                    yield
                    P.op("dve", lambda e, xm_=xm_: e.tensor_tensor(out=xm_, in0=ps_t[:, 0:1024], in1=xm_, op=ALU.add),
                         reads=[bankres[0], bankres[1], rxm], writes=[rxm])
                    (a0, r0) = st(sb + ti)
                    P.op("act", lambda e, xm_=xm_, a0=a0: e.activation(out=junk3, in_=xm_, func=AF.Square, accum_out=a0),
                         reads=[rxm], writes=[r_junk3, r0])
                    yield
                rl = [stat_res[sb + k] for k in range(ntile)]
                rm = [stat_res[sb + 4 + k] for k in range(ntile)]
                P.op("dve", lambda e: e.tensor_scalar(out=stat[:, sb + 4:sb + 4 + ntile], in0=stat[:, sb:sb + ntile],
                                                      scalar1=1.0 / D, scalar2=EPS, op0=ALU.mult, op1=ALU.add),
                     reads=rl, writes=rm)
                P.op("act", lambda e: e.activation(out=stat[:, sb + 4:sb + 4 + ntile], in_=stat[:, sb + 4:sb + 4 + ntile],
                                                   func=AF.Sqrt), reads=rm, writes=rm)
                P.op("dve", lambda e: e.reciprocal(out=stat[:, sb + 8:sb + 8 + ntile], in_=stat[:, sb + 4:sb + 4 + ntile]),
                     reads=rm, writes=[stat_res[sb + 8 + k] for k in range(ntile)])
                yield
                for ti in range(ntile):
                    (xm_, rxm) = xm[st_ * ntile + ti]
                    (h2_, rh2) = h2[ti % 2]
                    rs, rrs = st(sb + 8 + ti)
                    P.op("act", lambda e, xm_=xm_, h2_=h2_, rs=rs: e.activation(
                        out=h2_, in_=xm_, func=AF.Copy, scale=rs), reads=[rxm, rrs], writes=[rh2])
                    yield

                    def tr3(e, h2_=h2_):
                        pb = bank_bf(2)
                        for c in range(8):
                            ins = e.transpose(out=pb[:, c * 128:(c + 1) * 128], in_=h2_[:, c * 128:(c + 1) * 128],
                                              identity=ident)
                        return ins
                    P.op("pe", tr3, reads=[rh2, r_ident], writes=[bankres[2]])
                    yield
                    P.op("act", lambda e, ti=ti: e.copy(out=h2T_[:, :, ti * 128:(ti + 1) * 128],
                                                       in_=bank_bf(2).rearrange("p (c t) -> p c t", c=8)),
                         reads=[bankres[2]], writes=[r_h2T_])
                    yield

            def S2(sg_i):
                st_ = sg_i % 2
                (h2T_, r_h2T_) = h2Ts[st_]
                for j in range(NJ):
                    k = wcnt[0]
                    wcnt[0] += 1
                    (wg_, rwg), (wu_, rwu) = wgs[k % 3], wus[k % 3]
                    P.op("sp", lambda e, wg_=wg_, j=j: e.dma_start(
                        out=wg_, in_=s_wg[j].rearrange("p (c f) -> p c f", c=8)), reads=scr["wg"], writes=[rwg],
                        dma_sem=f"wg{k % 3}")
                    P.op("sp", lambda e, wu_=wu_, j=j: e.dma_start(
                        out=wu_, in_=s_wu[j].rearrange("p (c f) -> p c f", c=8)), reads=scr["wu"], writes=[rwu],
                        dma_sem=f"wu{k % 3}")
                    for hf in range(nhalf):
                        u = (j * nhalf + hf) % 2
                        gb, ub = 4 + u * 2, 5 + u * 2
                        ts_ = slice(hf * HW, (hf + 1) * HW)

                        def gmm(e, wg_=wg_, gb=gb, ts_=ts_):
                            for c in range(8):
                                ins = e.matmul(bank(gb)[:, 0:HW], lhsT=wg_[:, c, :], rhs=h2T_[:, c, ts_],
                                               start=(c == 0), stop=(c == 7))
                            return ins

                        def umm(e, wu_=wu_, ub=ub, ts_=ts_):
                            for c in range(8):
                                ins = e.matmul(bank(ub)[:, 0:HW], lhsT=wu_[:, c, :], rhs=h2T_[:, c, ts_],
                                               start=(c == 0), stop=(c == 7))
                            return ins
                        P.op("pe", gmm, reads=[rwg, r_h2T_], writes=[bankres[gb]])
                        P.op("pe", umm, reads=[rwu, r_h2T_], writes=[bankres[ub]])
                        (sg_, rsg) = sg[u]
                        P.op("act", lambda e, sg_=sg_, gb=gb: e.activation(out=sg_, in_=bank(gb)[:, 0:HW], func=AF.Silu),
                             reads=[bankres[gb]], writes=[rsg])
                        P.op("dve", lambda e, sg_=sg_, ub=ub, j=j, ts_=ts_: e.tensor_tensor(
                            out=actT[:, j, ts_], in0=sg_, in1=bank(ub)[:, 0:HW], op=ALU.mult),
                            reads=[rsg, bankres[ub]], writes=[r_actT])
                        yield

            def S3(sg_i):
                g0 = sg_i * T
                st_ = sg_i % 2
                sb = 48 + st_ * 8
                for q4 in range((ntile + 3) // 4):
                    tiles = list(range(q4 * 4, min(ntile, q4 * 4 + 4)))
                    for n in range(2):
                        for j in range(NJ):
                            k = wcnt[1]
                            wcnt[1] += 1
                            (wd_, rwd) = wds[k % 4]
                            P.op("sp", lambda e, wd_=wd_, j=j, n=n: e.dma_start(
                                out=wd_, in_=s_wd[j, :, n * 512:(n + 1) * 512]), reads=scr["wd"], writes=[rwd],
                                dma_sem=f"wd{k % 4}")

                            def dmm(e, wd_=wd_, j=j, tiles=tiles):
                                for a_, ti in enumerate(tiles):
                                    ins = e.matmul(bank(a_), lhsT=actT[:, j, ti * 128:(ti + 1) * 128], rhs=wd_,
                                                   start=(j == 0), stop=(j == NJ - 1))
                                return ins
                            P.op("pe", dmm, reads=[rwd, r_actT], writes=[bankres[a2] for a2 in range(len(tiles))])
                        for a_, ti in enumerate(tiles):
                            (xm_, rxm) = xm[st_ * ntile + ti]
                            P.op("dve", lambda e, xm_=xm_, a_=a_, n=n: e.tensor_tensor(
                                out=xm_[:, n * 512:(n + 1) * 512], in0=bank(a_), in1=xm_[:, n * 512:(n + 1) * 512],
                                op=ALU.add), reads=[bankres[a_], rxm], writes=[rxm])
                    nt_ = len(tiles)
                    for a_, ti in enumerate(tiles):
                        (xm_, rxm) = xm[st_ * ntile + ti]
                        (a0, r0) = st(sb + a_)
                        P.op("act", lambda e, xm_=xm_, a0=a0: e.activation(out=junk3, in_=xm_, func=AF.Square,
                                                                          accum_out=a0),
                             reads=[rxm], writes=[r_junk3, r0])
                    rl = [stat_res[sb + k] for k in range(nt_)]
                    rm = [stat_res[sb + 4 + k] for k in range(nt_)]
                    P.op("dve", lambda e: e.tensor_scalar(out=stat[:, sb + 4:sb + 4 + nt_], in0=stat[:, sb:sb + nt_],
                                                          scalar1=1.0 / D, scalar2=EPS, op0=ALU.mult, op1=ALU.add),
                         reads=rl, writes=rm)
                    P.op("act", lambda e: e.activation(out=stat[:, sb + 4:sb + 4 + nt_], in_=stat[:, sb + 4:sb + 4 + nt_],
                                                       func=AF.Sqrt), reads=rm, writes=rm)
                    P.op("dve", lambda e: e.reciprocal(out=stat[:, sb:sb + nt_], in_=stat[:, sb + 4:sb + 4 + nt_]),
                         reads=rm, writes=rl)
                    for a_, ti in enumerate(tiles):
                        (xm_, rxm) = xm[st_ * ntile + ti]
                        tok0 = g0 + ti * 128
                        rs, rrs = st(sb + a_)
                        P.op("dve", lambda e, xm_=xm_, rs=rs: e.scalar_tensor_tensor(
                            out=xm_, in0=xm_, scalar=rs, in1=gfin, op0=ALU.mult, op1=ALU.mult),
                            reads=[rxm, rrs, r_gfin], writes=[rxm])
                        P.op("pool", lambda e, xm_=xm_, tok0=tok0: e.dma_start(out=y_d[tok0:tok0 + 128, :], in_=xm_),
                             reads=[rxm], dma_sem=f"y{st_ * ntile + ti}")

            def interleave3(gens):
                gens = [g for g in gens if g is not None]
                while gens:
                    for g in list(gens):
                        try:
                            next(g)
                        except StopIteration:
                            gens.remove(g)

            interleave3([S1(0)])
            for sg_i in range(NSG):
                interleave3([S2(sg_i), S1(sg_i + 1) if sg_i + 1 < NSG else None])
                S3(sg_i)
            A.release(mseq)

        do_sequence(seqs[1])
        A.release(m0)
        do_sequence(seqs[0])

        block = es.enter_context(nc.Block())
        P.emit(block)
    return nc


def _rope_tables(pos, dim):
    inv = (1.0 / (np.float32(10000.0) ** (np.arange(0, dim, 2, dtype=np.float32) / np.float32(dim)))).astype(np.float32)
    ang = pos.astype(np.float32)[:, None] * inv[None, :]
    return np.cos(ang).astype(np.float32), np.sin(ang).astype(np.float32)


def _core_order(S, half):
    NQ = S // 2
    if half == 0:
        own = np.arange(0, NQ)
        halo = np.arange(NQ, NQ + 128)
        rest = np.arange(NQ + 128, S)
    else:
        own = np.arange(NQ, S)
        halo = np.arange(NQ - 128, NQ)
        rest = np.arange(0, NQ - 128)
    return np.concatenate([halo, own, rest]), own


_PROG_CACHE = {}


def _prepare(x_prompt, x_sample, g_mix, w_in, sink, cq_g, w_uq, ckv_g, w_ukv, w_o, g_ffn,
           w_gate, w_up, w_down, g_final):
    f = lambda a: np.ascontiguousarray(np.asarray(a, dtype=np.float32))
    x_prompt, x_sample = f(x_prompt), f(x_sample)
    SP, SS = x_prompt.shape[1], x_sample.shape[1]
    n_cores = 8
    key = (SP, SS)
    if key not in _PROG_CACHE:
        _PROG_CACHE[key] = build_program(SP, SS)
    nc = _PROG_CACHE[key]

    w_in0, w_uq0, w_ukv0, w_o0 = f(w_in)[0], f(w_uq)[0], f(w_ukv)[0], f(w_o)[0]
    w_gate0, w_up0, w_down0 = f(w_gate)[0], f(w_up)[0], f(w_down)[0]
    cols = []
    for h in range(HB):
        b = h * 96 + 64
        cols += list(range(b + 16, b + 32)) + list(range(b, b + 16))
    w_uqs0 = w_uq0[:, cols]
    w_uqc0 = np.ascontiguousarray(np.concatenate(
        [np.concatenate([w_uq0[:, h * 96:(h + 1) * 96], w_uqs0[:, h * 32:(h + 1) * 32]], axis=1) for h in range(HB)],
        axis=1))
    gcols = np.zeros((128, 21), np.float32)
    gcols[:, 0:8] = f(g_mix)[0].reshape(8, 128).T
    gcols[:, 8:16] = f(g_ffn)[0].reshape(8, 128).T
    gcols[:, 16:19] = f(cq_g)[0].reshape(3, 128).T
    gcols[:, 19:21] = f(ckv_g)[0].reshape(2, 128).T
    jj = np.arange(128)[:, None]
    ii = np.arange(128)[None, :]
    maskP = (jj >= ii).astype(np.float32)
    maskN = (jj <= ii).astype(np.float32)
    zero = np.zeros((128, 128), np.float32)

    in_maps = []
    owns = []
    for c in range(n_cores):
        b, half = c // 2, c % 2
        m = {}
        for tag, xs in (("p", x_prompt), ("s", x_sample)):
            S = xs.shape[1]
            NQ = S // 2
            order, own = _core_order(S, half)
            m["x" + tag] = np.ascontiguousarray(xs[b][order])
            cA, sA = _rope_tables(order[:NQ + 128], DA)
            m["ropeA_" + tag] = np.ascontiguousarray(np.concatenate([cA, sA], axis=1))
            cR, sR = _rope_tables(order, DR)
            m["ropeR_" + tag] = np.ascontiguousarray(np.concatenate([cR, sR], axis=1))
            cq, sq_ = _rope_tables(own, DR)
            m["qrc_" + tag] = np.ascontiguousarray(np.concatenate([cq, cq], axis=1).T)
            m["qrs_" + tag] = np.ascontiguousarray(np.concatenate([-sq_, sq_], axis=1).T)
        m["w_in"], m["w_uqc"], m["w_ukv"], m["w_o"] = w_in0, w_uqc0, w_ukv0, w_o0
        m["w_gate"], m["w_up"], m["w_down"] = w_gate0, w_up0, w_down0
        m["gcols"] = gcols
        m["g_final"] = f(g_final)
        m["sink"] = f(sink)[0]
        mk = np.stack([maskP, maskN, zero if half == 0 else maskP, maskN if half == 0 else zero], axis=1)
        m["masks"] = np.ascontiguousarray(mk)
        in_maps.append(m)

    return nc, in_maps, x_prompt, x_sample


def kernel(x_prompt, x_sample, g_mix, w_in, sink, cq_g, w_uq, ckv_g, w_ukv, w_o, g_ffn,
           w_gate, w_up, w_down, g_final):
    nc, in_maps, x_prompt, x_sample = _prepare(x_prompt, x_sample, g_mix, w_in, sink, cq_g, w_uq, ckv_g, w_ukv,
                                                w_o, g_ffn, w_gate, w_up, w_down, g_final)
    SP, SS = x_prompt.shape[1], x_sample.shape[1]
    n_cores = 8
    res = run_bass_kernel_spmd(nc, in_maps, core_ids=list(range(n_cores)))
    yp = np.empty_like(x_prompt)
    ys = np.empty_like(x_sample)
    for c in range(n_cores):
        b, half = c // 2, c % 2
        r = res.results[c]
        NQp, NQs = SP // 2, SS // 2
        yp[b, half * NQp:(half + 1) * NQp] = r["yp"]
        ys[b, half * NQs:(half + 1) * NQs] = r["ys"]
    return (yp, ys)
```
